# Optimizing a Trainium2 kernel written in Bass

```python
import jax, jax.numpy as jnp
from jax import lax
import numpy as np

D_MODEL = 1024
BATCH = 4
SEQ = 4096
DEPTH = 2
DEC_BATCH = 8
DEC_SEQ = 4096
PAST_LEN = 128

HEAD_DIM = 64
N_Q_HEADS = 8
N_KV_HEADS = 2
Q_PER_KV = N_Q_HEADS // N_KV_HEADS
ATTN_WIDTH = N_Q_HEADS * HEAD_DIM
KV_WIDTH = N_KV_HEADS * HEAD_DIM
FOURIER_WIDTH = D_MODEL - ATTN_WIDTH
N_FOURIER_GROUPS = 4
FOURIER_GROUP_DIM = FOURIER_WIDTH // N_FOURIER_GROUPS
MIX_WIDTH = ATTN_WIDTH + FOURIER_WIDTH
IN_WIDTH = ATTN_WIDTH + 2 * KV_WIDTH + ATTN_WIDTH + FOURIER_WIDTH + FOURIER_WIDTH
IN_SPLITS = (ATTN_WIDTH,
             ATTN_WIDTH + KV_WIDTH,
             ATTN_WIDTH + 2 * KV_WIDTH,
             2 * ATTN_WIDTH + 2 * KV_WIDTH,
             2 * ATTN_WIDTH + 2 * KV_WIDTH + FOURIER_WIDTH)
WINDOW = 128
BLOCK = 128
ROPE_THETA = 10000.0
EPS = 1e-6
NEG = -1e30

kernel_name = 'hymba_window_sink_gqa_fnet_encoder'


def rms_norm(x, gain):
    xf = x.astype(jnp.float32)
    xf = xf * lax.rsqrt(jnp.mean(xf * xf, axis=-1, keepdims=True) + EPS)
    return (xf * gain.astype(jnp.float32)).astype(x.dtype)


def rope(t):
    S = t.shape[1]
    half = HEAD_DIM // 2
    inv_freq = 1.0 / (ROPE_THETA ** (jnp.arange(half, dtype=jnp.float32) / half))
    ang = jnp.arange(S, dtype=jnp.float32)[:, None] * inv_freq[None, :]
    cos = jnp.cos(ang)[None, :, None, :]
    sin = jnp.sin(ang)[None, :, None, :]
    t1, t2 = t[..., :half], t[..., half:]
    return jnp.concatenate([t1 * cos - t2 * sin, t2 * cos + t1 * sin], axis=-1)


def banded_sink_attention(q, k, v, sink):
    B, S = q.shape[0], q.shape[1]
    nb = S // BLOCK
    qb = q.reshape(B, nb, BLOCK, N_KV_HEADS, Q_PER_KV, HEAD_DIM)
    pad = ((0, 0), (BLOCK, BLOCK), (0, 0), (0, 0))

    def bands(t):
        tb = jnp.pad(t, pad).reshape(B, nb + 2, BLOCK, N_KV_HEADS, HEAD_DIM)
        return jnp.concatenate([tb[:, :-2], tb[:, 1:-1], tb[:, 2:]], axis=2)

    kb, vb = bands(k), bands(v)
    s = jnp.einsum('bnqkgd,bnmkd->bnkgqm', qb, kb) * (HEAD_DIM ** -0.5)
    qi = jnp.arange(BLOCK)[:, None]
    mi = jnp.arange(3 * BLOCK)[None, :]
    rel = mi - BLOCK - qi
    key_pos = jnp.arange(nb)[:, None, None] * BLOCK + mi[None] - BLOCK
    valid = (jnp.abs(rel) <= WINDOW)[None] & (key_pos >= 0) & (key_pos < S)
    s = jnp.where(valid[None, :, None, None], s, NEG)
    sink_l = sink.astype(jnp.float32).reshape(N_KV_HEADS, Q_PER_KV)[None, None, :, :, None, None]
    m = jnp.maximum(jnp.max(s, axis=-1, keepdims=True), sink_l)
    p = jnp.exp(s - m)
    denom = jnp.sum(p, axis=-1, keepdims=True) + jnp.exp(sink_l - m)
    o = jnp.einsum('bnkgqm,bnmkd->bnqkgd', p / denom, vb)
    return o.reshape(B, S, ATTN_WIDTH)


def fourier_mix(u, w_lin):
    B, S = u.shape[0], u.shape[1]
    ug = u.astype(jnp.float32).reshape(B, S, N_FOURIER_GROUPS, FOURIER_GROUP_DIM)
    f = jnp.real(jnp.fft.fft2(ug, axes=(1, 3), norm='ortho'))
    out = jnp.einsum('bsgc,gcd->bsgd', f, w_lin.astype(jnp.float32))
    return out.reshape(B, S, FOURIER_WIDTH)


def hybrid_layer(x, g_norm, w_in, q_gain, k_gain, sink, w_four, w_out):
    B, S, _ = x.shape
    h = rms_norm(x, g_norm)
    z = h @ w_in
    q, k, v, g_attn, u, g_four = jnp.split(z, IN_SPLITS, axis=-1)
    q = rms_norm(q.reshape(B, S, N_Q_HEADS, HEAD_DIM), q_gain).astype(jnp.float32)
    k = rms_norm(k.reshape(B, S, N_KV_HEADS, HEAD_DIM), k_gain).astype(jnp.float32)
    v = v.reshape(B, S, N_KV_HEADS, HEAD_DIM).astype(jnp.float32)
    attn = banded_sink_attention(rope(q), rope(k), v, sink).astype(x.dtype)
    four = fourier_mix(u, w_four).astype(x.dtype)
    mixed = jnp.concatenate([jax.nn.silu(g_attn) * attn, jax.nn.silu(g_four) * four], axis=-1)
    return x + mixed @ w_out


def setup_inputs(seed: int = 0) -> dict:
    key = jax.random.key(seed)
    ks = jax.random.split(key, 9)
    f32 = jnp.float32
    x_prompt = jax.random.normal(ks[0], (BATCH, SEQ, D_MODEL), f32)
    x_sample = jax.random.normal(ks[1], (DEC_BATCH, DEC_SEQ, D_MODEL), f32)
    norm_gain = 1.0 + 0.02 * jax.random.normal(ks[2], (DEPTH, D_MODEL), f32)
    w_in = jax.random.normal(ks[3], (DEPTH, D_MODEL, IN_WIDTH), f32) * D_MODEL ** -0.5
    q_norm_gain = 1.0 + 0.02 * jax.random.normal(ks[4], (DEPTH, HEAD_DIM), f32)
    k_norm_gain = 1.0 + 0.02 * jax.random.normal(ks[5], (DEPTH, HEAD_DIM), f32)
    sink_logit = 0.5 * jax.random.normal(ks[6], (DEPTH, N_Q_HEADS), f32)
    w_fourier = jax.random.normal(ks[7], (DEPTH, N_FOURIER_GROUPS, FOURIER_GROUP_DIM, FOURIER_GROUP_DIM), f32) * FOURIER_GROUP_DIM ** -0.5
    w_out = jax.random.normal(ks[8], (DEPTH, MIX_WIDTH, D_MODEL), f32) * MIX_WIDTH ** -0.5
    return {'x_prompt': x_prompt, 'x_sample': x_sample, 'norm_gain': norm_gain, 'w_in': w_in,
            'q_norm_gain': q_norm_gain, 'k_norm_gain': k_norm_gain, 'sink_logit': sink_logit,
            'w_fourier': w_fourier, 'w_out': w_out}


def reference(x_prompt, x_sample, norm_gain, w_in, q_norm_gain, k_norm_gain, sink_logit, w_fourier, w_out):
    y_prompt = x_prompt
    y_sample = x_sample
    for l in range(DEPTH):
        params = (norm_gain[l], w_in[l], q_norm_gain[l], k_norm_gain[l], sink_logit[l], w_fourier[l], w_out[l])
        y_prompt = hybrid_layer(y_prompt, *params)
        y_sample = hybrid_layer(y_sample, *params)
    return (y_prompt, y_sample)
```

```python
import numpy as np
import ml_dtypes
from contextlib import ExitStack
import concourse.bass as bass
import concourse.mybir as mybir
from concourse.bass_utils import run_bass_kernel_spmd

F32 = mybir.dt.float32
BF16 = mybir.dt.bfloat16
ACT = mybir.ActivationFunctionType
ALU = mybir.AluOpType
AX = mybir.AxisListType

S = 4096
D = 1024
NT = 32
INW = 2304
EPS = 1e-6
N_CORES = 8


class Buf:
    __slots__ = ("name", "last_w", "readers", "dma_readers", "sem", "cnt")

    def __init__(self, name):
        self.name = name
        self.last_w = None
        self.readers = []
        self.dma_readers = []
        self.sem = None
        self.cnt = 0


class Op:
    __slots__ = ("idx", "eng", "fn", "dma", "dbuf", "dcnt", "deps", "signal",
                 "count", "waits", "barrier", "odeps", "cost")

    def __init__(self):
        self.deps = set()
        self.odeps = set()
        self.cost = 0.5
        self.signal = False
        self.count = None
        self.waits = []
        self.barrier = False
        self.dma = False
        self.dbuf = None
        self.dcnt = None
        self.fn = None


ENGS = ("sp", "act", "dve", "pool", "pe")


class Prog:
    def __init__(self, nc):
        self.nc = nc
        self.ops = []
        self.bufs = []
        self.stack = ExitStack()
        self.dma_since_barrier = []
        self.esem = {}
        self.last_compute = {}

    def buf(self, name):
        b = Buf(name)
        self.bufs.append(b)
        return b

    def op(self, eng, fn, reads=(), writes=(), dma=None):
        o = Op()
        o.idx = len(self.ops)
        o.eng = eng
        o.fn = fn
        if dma is not None:
            o.dma = True
            o.dbuf = dma
            dma.cnt += 16
            o.dcnt = dma.cnt
            self.dma_since_barrier.append(o.idx)
        else:
            self.last_compute[eng] = o.idx
        for r in reads:
            if r.last_w is not None:
                self._dep(o, r.last_w, True)
        for w in writes:
            if w.last_w is not None:
                self._dep(o, w.last_w, True)
            for i in w.readers:
                self._dep(o, i, False)
            for i in w.dma_readers:
                self._dep(o, i, False)
        for r in reads:
            if o.dma:
                r.dma_readers.append(o.idx)
            else:
                r.readers.append(o.idx)
        for w in writes:
            w.last_w = o.idx
            w.readers = []
            w.dma_readers = []
        self.ops.append(o)
        return o

    def _dep(self, o, i, raw):
        p = self.ops[i]
        if p.idx == o.idx:
            return
        o.odeps.add(i)
        if p.dma:
            o.deps.add(i)
            return
        if o.dma:
            p.signal = True
            o.deps.add(i)
            return
        if p.eng == o.eng and p.eng == "pe":
            return
        p.signal = True
        o.deps.add(i)

    def barrier(self):
        deps = set(self.dma_since_barrier)
        for e, i in self.last_compute.items():
            self.ops[i].signal = True
            deps.add(i)
        for e in ENGS:
            o = Op()
            o.idx = len(self.ops)
            o.eng = e
            o.barrier = True
            o.deps = set(deps)
            self.ops.append(o)
        self.dma_since_barrier = []
        self.last_compute = {}
        for b in self.bufs:
            b.last_w = None
            b.readers = []
            b.dma_readers = []

    class _Rec:
        def __init__(self):
            self.calls = []

        def __getattr__(self, name):
            def f(*a, **k):
                out = k.get("out", a[0] if a else None)
                self.calls.append((name, out))
                return self
            return f

        def then_inc(self, *a, **k):
            return self

    @staticmethod
    def _free_elems(ap):
        sh = tuple(ap.shape)
        n = 1
        for d in sh[1:]:
            n *= d
        return n, sh

    def _estimate(self, o):
        rec = Prog._Rec()
        o.fn(rec)
        c = 0.0
        for name, out in rec.calls:
            n, sh = Prog._free_elems(out)
            small = len(sh) >= 3 and sh[-1] <= 32
            if name == "dma_start":
                nbytes = n * sh[0] * (4 if out.dtype == F32 else 2)
                c += nbytes / 170e3
            elif o.eng == "pe":
                c += max(0.056, 0.012 + n * 0.00043)
            elif o.eng == "act":
                c += 0.2 + n * 0.00095
            elif o.eng == "dve":
                c += (0.1 + n * 0.0011) * (2.6 if small else 1.0)
            else:
                c += (0.15 + n * 0.0018) * (1.6 if small else 1.0)
        return c

    def schedule(self, window=48, lat=0.8, dma_lat=2.5):
        queues = {e: [] for e in ENGS}
        seg = []
        segs = []
        for o in self.ops:
            if o.barrier:
                if seg:
                    segs.append((seg, None))
                    seg = []
                segs.append((None, o))
            else:
                seg.append(o)
        if seg:
            segs.append((seg, None))
        self.sim_time = 0.0
        for ops, bar in segs:
            if bar is not None:
                queues[bar.eng].append(bar)
                continue
            for o in ops:
                o.cost = self._estimate(o)
            inseg = {o.idx for o in ops}
            pend = {e: [o for o in ops if o.eng == e] for e in ENGS}
            free = {e: 0.0 for e in ENGS}
            dma_free = 0.0
            done = {}
            nleft = len(ops)
            while nleft:
                best = None
                for e in ENGS:
                    cand = pend[e][:window]
                    for o in cand:
                        st = free[e]
                        ok = True
                        for d in o.odeps:
                            if d in inseg:
                                if d not in done:
                                    ok = False
                                    break
                                p = self.ops[d]
                                need_sync = (d in o.deps)
                                t = done[d] + (lat if need_sync else 0.0)
                                if t > st:
                                    st = t
                        if not ok:
                            continue
                        if best is None or st < best[0] - 1e-9 or (abs(st - best[0]) <= 1e-9 and o.idx < best[2].idx):
                            best = (st, e, o)
                        if st <= free[e] + 1e-9:
                            break
                assert best is not None, "scheduler deadlock"
                st, e, o = best
                if o.dma:
                    free[e] = st + 0.06
                    beg = max(st, dma_free)
                    dma_free = beg + o.cost
                    done[o.idx] = dma_free + dma_lat
                else:
                    free[e] = st + o.cost
                    done[o.idx] = free[e]
                pend[e].remove(o)
                queues[e].append(o)
                nleft -= 1
            self.sim_time += max(done.values()) if done else 0.0
        self.queues = queues

    def emit(self, sched=True):
        nc = self.nc
        st = self.stack
        for e in ENGS:
            self.esem[e] = st.enter_context(nc.semaphore("es_" + e))
        nsem = 0
        for b in self.bufs:
            if b.cnt > 0:
                b.sem = st.enter_context(nc.semaphore("ds%d_%s" % (nsem, b.name)))
                nsem += 1
        self.n_dma_sems = nsem
        if sched:
            self.schedule()
        else:
            self.queues = {e: [o for o in self.ops if o.eng == e] for e in ENGS}
        by_eng = self.queues
        cnt = {e: 0 for e in ENGS}
        for e in ENGS:
            for o in by_eng[e]:
                if o.barrier or o.dma:
                    continue
                if o.signal:
                    cnt[e] += 1
                    o.count = cnt[e]
        self.sig_counts = dict(cnt)
        for e in ENGS:
            w = {}
            for o in by_eng[e]:
                need = {}
                for i in o.deps:
                    p = self.ops[i]
                    if p.dma:
                        key = ("d", id(p.dbuf))
                        sem, val = p.dbuf.sem, p.dcnt
                    else:
                        key = ("e", p.eng)
                        sem, val = self.esem[p.eng], p.count
                    if key not in need or need[key][1] < val:
                        need[key] = (sem, val)
                for key, (sem, val) in need.items():
                    if w.get(key, 0) >= val:
                        continue
                    w[key] = val
                    o.waits.append((sem, val))

        def replay(name, e):
            for o in by_eng[name]:
                for sem, val in o.waits:
                    e.wait_ge(sem, val)
                if o.barrier:
                    continue
                inst = o.fn(e)
                if o.dma:
                    inst.then_inc(o.dbuf.sem, 16)
                elif o.signal:
                    inst.then_inc(self.esem[name], 1)

        with nc.Block() as block:
            @block.sync
            def _(e):
                replay("sp", e)

            @block.scalar
            def _(e):
                replay("act", e)

            @block.vector
            def _(e):
                replay("dve", e)

            @block.gpsimd
            def _(e):
                replay("pool", e)

            @block.tensor
            def _(e):
                replay("pe", e)
        st.close()


class T:
    __slots__ = ("ap", "b")

    def __init__(self, ap, b):
        self.ap = ap
        self.b = b


def _consts():
    bf = ml_dtypes.bfloat16
    c = {}
    c["ident"] = np.eye(128, dtype=np.float32).astype(bf)
    s1 = np.arange(128, dtype=np.float64)
    ang = 2 * np.pi * np.outer(s1, s1) / 128.0
    f1 = np.stack([np.cos(ang), -np.sin(ang)], axis=1) / 64.0
    f1 = f1.reshape(128, 2, 2, 64).transpose(0, 2, 1, 3)
    c["f1"] = np.ascontiguousarray(f1).reshape(128, 256).astype(np.float32).astype(bf)
    s2 = np.arange(32, dtype=np.float64)
    s1p = np.arange(128, dtype=np.float64)
    s2p = np.arange(32, dtype=np.float64)
    th = 2 * np.pi * (s2[:, None, None] * s1p[None, :, None] / 4096.0
                      + s2[:, None, None] * s2p[None, None, :] / 32.0)
    tr, ti = np.cos(th), -np.sin(th)
    f2 = np.concatenate([-ti, tr, ti], axis=2)
    f2 = f2.reshape(32, 128 * 96)
    c["f2"] = np.tile(f2, (4, 1)).astype(np.float32).astype(bf)
    a2 = 2 * np.pi * np.outer(s1, s1) / 128.0
    c["cs128"] = (np.concatenate([np.cos(a2), np.sin(a2)], axis=1) / np.sqrt(128.0)
                  ).astype(np.float32).astype(bf)
    half = 32
    inv = (1.0 / (10000.0 ** (np.arange(half, dtype=np.float32) / half))).astype(np.float32)
    pos = np.arange(S, dtype=np.float32)
    angr = (pos[:, None] * inv[None, :]).astype(np.float32)
    cosr = np.cos(angr).astype(np.float32).reshape(NT, 128, half).transpose(1, 0, 2)
    sinr = np.sin(angr).astype(np.float32).reshape(NT, 128, half).transpose(1, 0, 2)
    c["cosr"] = np.ascontiguousarray(cosr).reshape(128, NT * half)
    c["sinr"] = np.ascontiguousarray(sinr).reshape(128, NT * half)
    j = np.arange(128)[:, None]
    i = np.arange(128)[None, :]
    ml = (j >= i).astype(np.float32)
    mr = (j <= i).astype(np.float32)
    c["masks"] = np.concatenate([np.tile(ml, (1, 4)), np.tile(mr, (1, 4))], axis=1).astype(bf)
    return c


def build(n_slots=2, n_layers=2, debug=False, stop=None):
    nc = bass.Bass("TRN2", target_bir_lowering=False)
    P = Prog(nc)
    st = P.stack

    def din(name, shape, dt=F32):
        return nc.dram_tensor(name, list(shape), dt, kind="ExternalInput").ap()

    x_d = din("x", [n_slots, S, D])
    w_in_d = din("w_in", [2, D, INW])
    w_out_d = din("w_out", [2, D, D])
    gd_d = din("gain_d", [2, 128, 8])
    qkg_d = din("qkg", [2, 1, 640])
    sink_d = din("sink", [2, 1, 8])
    wf_d = din("w_four", [2, 4, 128, 128])
    ident_d = din("ident", [128, 128], BF16)
    f1_d = din("f1", [128, 256], BF16)
    f2_d = din("f2", [128, 12288], BF16)
    cs_d = din("cs128", [128, 256], BF16)
    cos_d = din("cosr", [128, NT * 32])
    sin_d = din("sinr", [128, NT * 32])
    masks_d = din("masks", [128, 1024], BF16)
    y_d = nc.dram_tensor("y", [n_slots, S, D], F32, kind="ExternalOutput").ap()
    u_d = nc.dram_tensor("u_scr", [S, 512], BF16, kind="Internal").ap()
    sga_d = nc.dram_tensor("sga_scr", [S, 512], BF16, kind="Internal").ap()
    y1_d = nc.dram_tensor("y1_scr", [n_slots, S, D], F32, kind="Internal").ap()
    wc_d = nc.dram_tensor("w_in_cache", [128, 8 * INW], BF16, kind="Internal").ap()
    dbg = {}
    if debug:
        dbg["qkt"] = nc.dram_tensor("dbg_qkt", [128, 6 * S], BF16, kind="ExternalOutput").ap()
        dbg["vall"] = nc.dram_tensor("dbg_vall", [128, NT * 130], BF16, kind="ExternalOutput").ap()
        dbg["sgt"] = nc.dram_tensor("dbg_sgt", [128, 4 * S], BF16, kind="ExternalOutput").ap()
        dbg["u"] = nc.dram_tensor("dbg_u", [S, 512], BF16, kind="ExternalOutput").ap()
        dbg["sga"] = nc.dram_tensor("dbg_sga", [S, 512], BF16, kind="ExternalOutput").ap()
        dbg["ypart"] = nc.dram_tensor("dbg_ypart", [S, D], F32, kind="ExternalOutput").ap()
        dbg["mixf"] = nc.dram_tensor("dbg_mixf", [128, 4 * S], BF16, kind="ExternalOutput").ap()

    ARENA_BYTES = 207 * 1024
    arena = st.enter_context(nc.sbuf_tensor("arena", [128, ARENA_BYTES // 2], BF16))
    astate = {"off": 0}

    def alloc(name, n, dt):
        nbytes = n * (2 if dt == BF16 else 4)
        off = (astate["off"] + 63) // 64 * 64
        assert off + nbytes <= ARENA_BYTES, (name, off, nbytes)
        astate["off"] = off + nbytes
        if dt == BF16:
            ap = arena[:, off // 2: off // 2 + n]
        else:
            ap = arena[:, off // 2: off // 2 + 2 * n].bitcast(F32)
        return T(ap, P.buf(name))

    psum = [st.enter_context(nc.psum_tensor("ps%d" % i, [128, 512], F32)) for i in range(8)]

    def pbank(i, name):
        return T(psum[i][:, :], P.buf(name))

    def pbf(t):
        return t.ap.bitcast(BF16)

    ident = alloc("ident", 128, BF16)
    f1 = alloc("f1", 256, BF16)
    cs128 = alloc("cs128", 256, BF16)
    masks = alloc("masks", 1024, BF16)
    cosr = alloc("cosr", NT * 32, F32)
    sinr = alloc("sinr", NT * 32, F32)
    g640 = alloc("g640", 640, F32)
    gd = alloc("gd", 8, F32)
    sk = alloc("sk", 8, F32)
    esink = alloc("esink", 8, F32)
    wcs = alloc("wcs", 4 * 2 * 128, BF16)
    w_out_bf = alloc("w_out_bf", 8 * 1024, BF16)
    sgT = alloc("sgT", 4 * S, BF16)
    P_MARK = astate["off"]

    def dma(out, in_, t, write):
        if write:
            P.op("sp", lambda e: e.dma_start(out=out, in_=in_), writes=[t.b], dma=t.b)
        else:
            P.op("pool", lambda e: e.dma_start(out=out, in_=in_), reads=[t.b], dma=t.b)

    dma(ident.ap, ident_d[:, :], ident, True)
    dma(f1.ap, f1_d[:, :], f1, True)
    dma(cs128.ap, cs_d[:, :], cs128, True)
    dma(masks.ap, masks_d[:, :], masks, True)
    dma(cosr.ap, cos_d[:, :], cosr, True)
    dma(sinr.ap, sin_d[:, :], sinr, True)

    w_out3 = w_out_bf.ap.rearrange("p (j n) -> p j n", j=8)
    sgT3 = sgT.ap.rearrange("p (j s) -> p j s", j=4)

    wcs4 = wcs.ap.rearrange("p (g r d) -> p g r d", g=4, r=2)

    for layer in range(n_layers):
        for slot in range(n_slots):
            xsrc = x_d[slot] if layer == 0 else y1_d[slot]
            ydst = y1_d[slot] if layer < n_layers - 1 else y_d[slot]
            last = (slot == n_slots - 1 and layer == n_layers - 1)

            astate["off"] = P_MARK
            w_in_bf = alloc("w_in_bf", 8 * INW, BF16)
            w_in3 = w_in_bf.ap.rearrange("p (k n) -> p k n", k=8)
            A_MARK = astate["off"]
            mTa = T(w_in_bf.ap[:, 0:4 * S], P.buf("mTa"))
            mTa3 = mTa.ap.rearrange("p (j s) -> p j s", j=4)
            ws = [alloc("ws%d" % i, INW, F32) for i in range(2)]
            wf32 = alloc("wf32", 512, F32)
            wfb = alloc("wfb", 512, BF16)
            pw0 = pbank(0, "pw0")
            pw1 = pbank(1, "pw1")

            if slot == 0:
                dma(gd.ap, gd_d[layer], gd, True)
                dma(g640.ap, qkg_d[layer, 0:1, :].partition_broadcast(128), g640, True)
                dma(sk.ap, sink_d[layer, 0:1, :].partition_broadcast(128), sk, True)
                P.op("act", lambda e: e.activation(out=esink.ap, in_=sk.ap, func=ACT.Exp),
                     reads=[sk.b], writes=[esink.b])
                for kc in range(8):
                    w = ws[kc % 2]
                    dma(w.ap, w_in_d[layer, kc * 128:(kc + 1) * 128, :], w, True)
                    if kc % 2 == 0:
                        P.op("dve", lambda e, w=w, kc=kc: e.tensor_scalar(
                            out=w_in3[:, kc, :], in0=w.ap, scalar1=gd.ap[:, kc:kc + 1], scalar2=None,
                            op0=ALU.mult), reads=[w.b, gd.b], writes=[w_in_bf.b])
                    else:
                        P.op("act", lambda e, w=w, kc=kc: e.activation(
                            out=w_in3[:, kc, :], in_=w.ap, func=ACT.Copy, scale=gd.ap[:, kc:kc + 1]),
                            reads=[w.b, gd.b], writes=[w_in_bf.b])
                dma(wf32.ap.rearrange("p (g d) -> p g d", g=4),
                    wf_d[layer].rearrange("g c d -> c g d"), wf32, True)
                P.op("dve", lambda e: e.tensor_copy(out=wfb.ap, in_=wf32.ap), reads=[wf32.b], writes=[wfb.b])

                def wfmm(e):
                    i = None
                    for g in range(4):
                        e.matmul(pw0.ap[:, 128 * g:128 * g + 128], cs128.ap[:, 0:128],
                                 wfb.ap[:, 128 * g:128 * g + 128], start=True, stop=True)
                        i = e.matmul(pw1.ap[:, 128 * g:128 * g + 128], cs128.ap[:, 128:256],
                                     wfb.ap[:, 128 * g:128 * g + 128], start=True, stop=True)
                    return i
                P.op("pe", wfmm, reads=[cs128.b, wfb.b], writes=[pw0.b, pw1.b])
                P.op("dve", lambda e: e.tensor_copy(out=wcs4[:, :, 0, :],
                                                    in_=pw0.ap.rearrange("p (g d) -> p g d", g=4)),
                     reads=[pw0.b], writes=[wcs.b])
                P.op("act", lambda e: e.activation(out=wcs4[:, :, 1, :],
                                                   in_=pw1.ap.rearrange("p (g d) -> p g d", g=4),
                                                   func=ACT.Copy),
                     reads=[pw1.b], writes=[wcs.b])

                for i in range(4):
                    P.op("sp", lambda e, i=i: e.dma_start(
                        out=wc_d[:, 2 * INW * i:2 * INW * (i + 1)],
                        in_=w_in_bf.ap[:, 2 * INW * i:2 * INW * (i + 1)]), reads=[w_in_bf.b], dma=w_in_bf.b)
            else:
                for i in range(4):
                    P.op("sp", lambda e, i=i: e.dma_start(
                        out=w_in_bf.ap[:, 2 * INW * i:2 * INW * (i + 1)],
                        in_=wc_d[:, 2 * INW * i:2 * INW * (i + 1)]), writes=[w_in_bf.b], dma=w_in_bf.b)
            P.barrier()
            if stop == "W":
                break

            astate["off"] = A_MARK
            qkt = alloc("qkt", 6 * S, BF16)
            qkt3 = qkt.ap.rearrange("p (j s) -> p j s", j=6)
            vall = alloc("vall", NT * 130, BF16)
            vall4 = vall.ap.rearrange("p (t k d) -> p t k d", t=NT, k=2)
            B_MARK = astate["off"]
            xt = [alloc("xt%d" % i, 1024, F32) for i in range(3)]
            hb = [alloc("hb%d" % i, 1024, BF16) for i in range(2)]
            hT = [alloc("hT%d" % i, 1024, BF16) for i in range(2)]
            stt = [alloc("stt%d" % i, 16, F32) for i in range(2)]
            qk32 = [alloc("qk32_%d" % i, 640, F32) for i in range(2)]
            sq = [alloc("sq%d" % i, 640, F32) for i in range(2)]
            hst = [alloc("hst%d" % i, 32, F32) for i in range(2)]
            t1 = alloc("t1", 640, F32)
            rm = [alloc("rm%d" % i, 320, F32) for i in range(4)]
            qrot = [alloc("qrot%d" % i, 640, BF16) for i in range(2)]
            ea = [alloc("ea%d" % i, 512, F32) for i in range(2)]
            sga = [alloc("sga%d" % i, 512, BF16) for i in range(1)] * 2
            ub = [alloc("ub%d" % i, 512, BF16) for i in range(1)] * 2
            sgf = [alloc("sgf%d" % i, 512, BF16) for i in range(3)]
            p_trh = pbank(0, "p_trh")
            p_q = pbank(1, "p_q")
            p_kv = pbank(2, "p_kv")
            p_ga = pbank(3, "p_ga")
            p_u = pbank(4, "p_u")
            p_gf = pbank(5, "p_gf")
            p_trq = pbank(6, "p_trq")
            p_trg = pbank(7, "p_trg")

            qr_halves = {}
            P.op("pool", lambda e: e.memset(vall.ap, 1.0), writes=[vall.b])
            P.op("pool", lambda e: e.memset(qkt.ap[:, 4 * S:6 * S], 0.0), writes=[qkt.b])

            def A_load(t):
                dma(xt[t % 3].ap, xsrc[t * 128:(t + 1) * 128, :], xt[t % 3], True)

            def A_pre1(t):
                x_, s_, h_ = xt[t % 3], stt[t % 2], hb[t % 2]
                P.op("act", lambda e: e.activation(out=h_.ap, in_=x_.ap, func=ACT.Square,
                                                   accum_out=s_.ap[:, 0:1]),
                     reads=[x_.b], writes=[h_.b, s_.b])
                P.op("dve", lambda e: e.tensor_scalar(out=s_.ap[:, 1:2], in0=s_.ap[:, 0:1], scalar1=1.0 / D,
                                                      scalar2=EPS, op0=ALU.mult, op1=ALU.add),
                     reads=[s_.b], writes=[s_.b])
                P.op("act", lambda e: e.activation(out=s_.ap[:, 2:3], in_=s_.ap[:, 1:2], func=ACT.Ln),
                     reads=[s_.b], writes=[s_.b])
                P.op("act", lambda e: e.activation(out=s_.ap[:, 3:4], in_=s_.ap[:, 2:3], func=ACT.Exp, scale=-0.5),
                     reads=[s_.b], writes=[s_.b])
                P.op("pool", lambda e: e.tensor_tensor(out=h_.ap, in0=x_.ap,
                                                       in1=s_.ap[:, 3:4].to_broadcast([128, 1024]), op=ALU.mult),
                     reads=[x_.b, s_.b], writes=[h_.b])

            def A_pre2(t):
                h_, hT_ = hb[t % 2], hT[t % 2]

                def trh(e):
                    i = None
                    pb = pbf(p_trh)
                    for k in range(8):
                        i = e.transpose(pb[:, 128 * k:128 * k + 128], h_.ap[:, 128 * k:128 * k + 128], ident.ap)
                    return i
                P.op("pe", trh, reads=[h_.b, ident.b], writes=[p_trh.b])
                P.op("dve", lambda e: e.tensor_copy(out=hT_.ap, in_=pbf(p_trh)), reads=[p_trh.b], writes=[hT_.b])

            def gate(pt, ea_, out_):
                P.op("act", lambda e: e.activation(out=ea_.ap, in_=pt.ap, func=ACT.Exp, scale=-1.0),
                     reads=[pt.b], writes=[ea_.b])
                P.op("act", lambda e: e.activation(out=ea_.ap, in_=ea_.ap, func=ACT.Ln, bias=1.0),
                     reads=[ea_.b], writes=[ea_.b])
                P.op("act", lambda e: e.activation(out=ea_.ap, in_=ea_.ap, func=ACT.Exp, scale=-1.0),
                     reads=[ea_.b], writes=[ea_.b])
                P.op("dve", lambda e: e.tensor_tensor(out=out_.ap, in0=pt.ap, in1=ea_.ap, op=ALU.mult),
                     reads=[pt.b, ea_.b], writes=[out_.b])

            def A_front(t, part):
                hT_ = hT[t % 2]
                hT3 = hT_.ap.rearrange("p (k n) -> p k n", k=8)
                q32, sga_, sgf_, ub_ = qk32[t % 2], sga[t % 2], sgf[t % 3], ub[t % 2]

                def proj(pt, c0, n):
                    def f(e):
                        i = None
                        for k in range(8):
                            i = e.matmul(pt.ap[:, 0:n], hT3[:, k, :], w_in3[:, k, c0:c0 + n],
                                         start=(k == 0), stop=(k == 7))
                        return i
                    P.op("pe", f, reads=[hT_.b, w_in_bf.b], writes=[pt.b])
                if part == 1:
                    proj(p_ga, 768, 512)
                    gate(p_ga, ea[0], sga_)
                    dma(sga_d[128 * t:128 * t + 128, :], sga_.ap, sga_, False)
                    proj(p_u, 1280, 512)
                    P.op("dve", lambda e: e.tensor_copy(out=ub_.ap, in_=p_u.ap),
                         reads=[p_u.b], writes=[ub_.b])
                    dma(u_d[128 * t:128 * t + 128, :], ub_.ap, ub_, False)
                    proj(p_gf, 1792, 512)
                    gate(p_gf, ea[1], sgf_)
                    return
                sq_ = sq[t % 2]
                proj(p_q, 0, 512)
                P.op("dve", lambda e: e.tensor_tensor(out=q32.ap[:, 0:512], in0=p_q.ap, in1=g640.ap[:, 0:512],
                                                      op=ALU.mult),
                     reads=[p_q.b, g640.b], writes=[q32.b])
                P.op("act", lambda e: e.activation(out=sq_.ap[:, 0:512], in_=p_q.ap, func=ACT.Square),
                     reads=[p_q.b, q32.b], writes=[sq_.b])
                proj(p_kv, 512, 256)
                P.op("dve", lambda e: e.tensor_tensor(out=q32.ap[:, 512:640], in0=p_kv.ap[:, 0:128],
                                                      in1=g640.ap[:, 512:640], op=ALU.mult),
                     reads=[p_kv.b, g640.b], writes=[q32.b])
                P.op("act", lambda e: e.activation(out=sq_.ap[:, 512:640], in_=p_kv.ap[:, 0:128], func=ACT.Square),
                     reads=[p_kv.b, q32.b], writes=[sq_.b])
                P.op("act", lambda e: e.activation(
                    out=vall4[:, t, :, 0:64], in_=p_kv.ap[:, 128:256].rearrange("p (k d) -> p k d", k=2),
                    func=ACT.Copy), reads=[p_kv.b, q32.b], writes=[vall.b])

            def A_back1(t):
                q32, hs_, qr_, sq_ = qk32[t % 2], hst[t % 2], qrot[t % 2], sq[t % 2]
                P.op("dve", lambda e: e.tensor_reduce(
                    out=hs_.ap[:, 0:10], in_=sq_.ap.rearrange("p (h d) -> p h d", h=10), axis=AX.X, op=ALU.add),
                    reads=[sq_.b], writes=[hs_.b])
                P.op("dve", lambda e: e.tensor_scalar(
                    out=hs_.ap[:, 10:20], in0=hs_.ap[:, 0:10], scalar1=1.0 / 64, scalar2=EPS,
                    op0=ALU.mult, op1=ALU.add), reads=[hs_.b], writes=[hs_.b])
                P.op("act", lambda e: e.activation(out=hs_.ap[:, 0:10], in_=hs_.ap[:, 10:20], func=ACT.Ln),
                     reads=[hs_.b], writes=[hs_.b])
                P.op("act", lambda e: e.activation(out=hs_.ap[:, 20:30], in_=hs_.ap[:, 0:10], func=ACT.Exp,
                                                   scale=-0.5), reads=[hs_.b], writes=[hs_.b])
                t1_4 = t1.ap[:, 0:512].rearrange("p (hh kv d) -> p kv hh d", hh=4, kv=2)
                q4 = q32.ap[:, 0:512].rearrange("p (kv hh d) -> p kv hh d", kv=2, hh=4)

                def qscale(e):
                    rq = hs_.ap[:, 20:28].rearrange("p (kv hh) -> p kv hh", kv=2)
                    for kv in range(2):
                        e.tensor_tensor(out=t1_4[:, kv], in0=q4[:, kv],
                                        in1=rq[:, kv].unsqueeze(2).to_broadcast([128, 4, 64]), op=ALU.mult)
                    return e.tensor_tensor(
                        out=t1.ap[:, 512:640].rearrange("p (h d) -> p h d", h=2),
                        in0=q32.ap[:, 512:640].rearrange("p (h d) -> p h d", h=2),
                        in1=hs_.ap[:, 28:30].unsqueeze(2).to_broadcast([128, 2, 64]), op=ALU.mult)
                P.op("dve", qscale, reads=[q32.b, hs_.b], writes=[t1.b])
                t13 = t1.ap.rearrange("p (h d) -> p h d", h=10)
                ta, tb = t13[:, :, 0:32], t13[:, :, 32:64]
                cosb = cosr.ap[:, 32 * t:32 * t + 32].unsqueeze(1).to_broadcast([128, 10, 32])
                sinb = sinr.ap[:, 32 * t:32 * t + 32].unsqueeze(1).to_broadcast([128, 10, 32])
                rm3 = [r.ap.rearrange("p (h d) -> p h d", h=10) for r in rm]
                qr3 = qr_.ap.rearrange("p (h d) -> p h d", h=10)
                b_qa, b_qb = P.buf("qra"), P.buf("qrb")

                def rope_a(e):
                    e.tensor_tensor(out=rm3[0], in0=ta, in1=cosb, op=ALU.mult)
                    e.tensor_tensor(out=rm3[1], in0=tb, in1=sinb, op=ALU.mult)
                    return e
                P.op("dve", lambda e: e.tensor_tensor(out=rm3[0], in0=ta, in1=cosb, op=ALU.mult),
                     reads=[t1.b, cosr.b], writes=[rm[0].b])
                P.op("dve", lambda e: e.tensor_tensor(out=rm3[1], in0=tb, in1=sinb, op=ALU.mult),
                     reads=[t1.b, sinr.b], writes=[rm[1].b])
                P.op("pool", lambda e: e.tensor_tensor(out=rm3[2], in0=tb, in1=cosb, op=ALU.mult),
                     reads=[t1.b, cosr.b], writes=[rm[2].b])
                P.op("pool", lambda e: e.tensor_tensor(out=rm3[3], in0=ta, in1=sinb, op=ALU.mult),
                     reads=[t1.b, sinr.b], writes=[rm[3].b])
                P.op("dve", lambda e: e.tensor_tensor(out=qr3[:, :, 0:32], in0=rm3[0], in1=rm3[1],
                                                      op=ALU.subtract),
                     reads=[rm[0].b, rm[1].b, qr_.b], writes=[b_qa])
                P.op("pool", lambda e: e.tensor_tensor(out=qr3[:, :, 32:64], in0=rm3[2], in1=rm3[3], op=ALU.add),
                     reads=[rm[2].b, rm[3].b, qr_.b], writes=[b_qb])
                qr_halves[t % 2] = (b_qa, b_qb)

            def A_back2(t):
                qr_, sgf_ = qrot[t % 2], sgf[t % 3]

                def trq(e):
                    i = None
                    pb = pbf(p_trq)
                    for j in range(5):
                        i = e.transpose(pb[:, 128 * j:128 * j + 128], qr_.ap[:, 128 * j:128 * j + 128], ident.ap)
                    return i
                P.op("pe", trq, reads=list(qr_halves[t % 2]) + [ident.b], writes=[p_trq.b, qr_.b])
                def qcopy(e):
                    pb = pbf(p_trq)
                    e.activation(out=qkt3[:, 0:4, 128 * t:128 * t + 128],
                                 in_=pb[:, 0:512].rearrange("p (j n) -> p j n", j=4), func=ACT.Copy)
                    e.activation(out=qkt3[0:64, 4, 128 * t:128 * t + 128], in_=pb[0:64, 512:640], func=ACT.Copy)
                    return e.activation(out=qkt3[64:128, 5, 128 * t:128 * t + 128], in_=pb[64:128, 512:640],
                                        func=ACT.Copy)
                P.op("act", qcopy, reads=[p_trq.b], writes=[qkt.b])

                def trg(e):
                    i = None
                    pb = pbf(p_trg)
                    for j in range(4):
                        i = e.transpose(pb[:, 128 * j:128 * j + 128], sgf_.ap[:, 128 * j:128 * j + 128], ident.ap)
                    return i
                P.op("pe", trg, reads=[sgf_.b, ident.b], writes=[p_trg.b])
                P.op("act", lambda e: e.activation(
                    out=sgT3[:, :, 128 * t:128 * t + 128],
                    in_=pbf(p_trg)[:, 0:512].rearrange("p (j n) -> p j n", j=4), func=ACT.Copy),
                    reads=[p_trg.b], writes=[sgT.b])

            A_load(0)
            A_load(1)
            A_load(2)
            A_pre1(0)
            A_pre1(1)
            A_pre2(0)
            for i in range(NT + 2):
                if i + 3 < NT:
                    A_load(i + 3)
                if i < NT:
                    A_front(i, 0)
                if i + 2 < NT:
                    A_pre1(i + 2)
                if i + 1 < NT:
                    A_pre2(i + 1)
                if i < NT:
                    A_front(i, 1)
                if 0 <= i - 1 < NT:
                    A_back1(i - 1)
                if 0 <= i - 2 < NT:
                    A_back2(i - 2)
            P.barrier()
            if debug and slot == 0 and layer == 0:
                dma(dbg["qkt"][:, :], qkt.ap, qkt, False)
                dma(dbg["vall"][:, :], vall.ap, vall, False)
                dma(dbg["sgt"][:, :], sgT.ap, sgT, False)
                P.op("sp", lambda e: e.dma_start(out=dbg["u"][:, :], in_=u_d[:, :]), reads=[], dma=ub[0].b)
                P.op("sp", lambda e: e.dma_start(out=dbg["sga"][:, :], in_=sga_d[:, :]), reads=[], dma=ub[1].b)
                P.barrier()

            if stop == "A":
                break
            astate["off"] = B_MARK
            sgl = [alloc("sgl%d" % i, 512, BF16) for i in range(2)]
            pt_ = [[[alloc("pt%d_%d_%d" % (b_, kv, i), 512, BF16) for i in range(3)] for kv in range(2)]
                   for b_ in range(2)]
            dn = alloc("dn", 16, F32)
            on = alloc("on", 512, F32)
            mx = [alloc("mx%d" % i, 512, BF16) for i in range(2)]
            p_s = [pbank(i, "p_s%d" % i) for i in range(3)]
            p_o = [pbank(3 + i, "p_o%d" % i) for i in range(2)]
            p_trm = [pbank(5 + i, "p_trm%d" % i) for i in range(2)]

            if slot == 0:
                wo_st = [alloc("wo_st%d" % i, 1024, F32) for i in range(2)]
                for jc in range(8):
                    w = wo_st[jc % 2]
                    dma(w.ap, w_out_d[layer, jc * 128:(jc + 1) * 128, :], w, True)
                    P.op("pool", lambda e, w=w, jc=jc: e.tensor_copy(out=w_out3[:, jc, :], in_=w.ap),
                         reads=[w.b], writes=[w_out_bf.b])

            def kbs_of(n):
                return [kb for kb in (n - 1, n, n + 1) if 0 <= kb < NT]

            def B_s0(n, kv):
                lo = 64 * kv
                for i, kb in enumerate(kbs_of(n)):
                    P.op("pe", lambda e, i=i, kb=kb: e.matmul(
                        p_s[i].ap, qkt3[:, 4 + kv, 128 * kb:128 * kb + 128],
                        qkt3[:, 0:4, 128 * n:128 * n + 128], start=True, stop=True),
                        reads=[qkt.b], writes=[p_s[i].b])
                    pti = pt_[n % 2][kv][i]
                    P.op("act", lambda e, i=i, pti=pti: e.activation(
                        out=pti.ap, in_=p_s[i].ap, func=ACT.Exp, scale=0.125),
                        reads=[p_s[i].b], writes=[pti.b])
                    if kb != n:
                        mk = masks.ap[:, 0:512] if kb == n - 1 else masks.ap[:, 512:1024]
                        P.op("dve", lambda e, pti=pti, mk=mk: e.tensor_tensor(
                            out=pti.ap, in0=pti.ap, in1=mk, op=ALU.mult),
                            reads=[pti.b, masks.b], writes=[pti.b])

            def B_s1(n):
                kbs = kbs_of(n)
                sg_, mx_ = sgl[n % 2], mx[n % 2]
                for kv in range(2):
                    def pv(e, kv=kv):
                        ins = None
                        for hh in range(4):
                            for i, kb in enumerate(kbs):
                                ins = e.matmul(p_o[kv].ap[:, 65 * hh:65 * hh + 65],
                                               pt_[n % 2][kv][i].ap[:, 128 * hh:128 * hh + 128],
                                               vall4[:, kb, kv, :], start=(i == 0), stop=(i == len(kbs) - 1))
                        return ins
                    P.op("pe", pv, reads=[pt_[n % 2][kv][i].b for i in range(len(kbs))] + [vall.b],
                         writes=[p_o[kv].b])
                for kv in range(2):
                    po3 = p_o[kv].ap[:, 0:260].rearrange("p (h d) -> p h d", h=4)
                    P.op("dve", lambda e, po3=po3, kv=kv: e.tensor_tensor(
                        out=dn.ap[:, 4 * kv:4 * kv + 4].unsqueeze(2), in0=po3[:, :, 64:65],
                        in1=esink.ap[:, 4 * kv:4 * kv + 4].unsqueeze(2), op=ALU.add),
                        reads=[p_o[kv].b, esink.b], writes=[dn.b])
                P.op("dve", lambda e: e.reciprocal(out=dn.ap[:, 8:16], in_=dn.ap[:, 0:8]),
                     reads=[dn.b], writes=[dn.b])
                on3 = on.ap.rearrange("p (h d) -> p h d", h=8)
                for kv in range(2):
                    po3 = p_o[kv].ap[:, 0:260].rearrange("p (h d) -> p h d", h=4)
                    P.op("dve", lambda e, po3=po3, kv=kv: e.tensor_tensor(
                        out=on3[:, 4 * kv:4 * kv + 4, :], in0=po3[:, :, 0:64],
                        in1=dn.ap[:, 8 + 4 * kv:12 + 4 * kv].unsqueeze(2).to_broadcast([128, 4, 64]),
                        op=ALU.mult), reads=[p_o[kv].b, dn.b], writes=[on.b])
                P.op("dve", lambda e: e.tensor_tensor(out=mx_.ap, in0=on.ap, in1=sg_.ap, op=ALU.mult),
                     reads=[on.b, sg_.b], writes=[mx_.b])

            def B_s2(n):
                mx_ = mx[n % 2]
                ptm = p_trm[n % 2]

                def trm(e):
                    i = None
                    pb = pbf(ptm)
                    for j in range(4):
                        i = e.transpose(pb[:, 128 * j:128 * j + 128], mx_.ap[:, 128 * j:128 * j + 128], ident.ap)
                    return i
                P.op("pe", trm, reads=[mx_.b, ident.b], writes=[ptm.b])
                P.op("act", lambda e: e.activation(
                    out=mTa3[:, :, 128 * n:128 * n + 128],
                    in_=pbf(ptm)[:, 0:512].rearrange("p (j n) -> p j n", j=4), func=ACT.Copy),
                    reads=[ptm.b], writes=[mTa.b])

            def B_loadsg(n):
                dma(sgl[n % 2].ap, sga_d[n * 128:(n + 1) * 128, :], sgl[n % 2], True)

            B_loadsg(0)
            for i in range(NT + 2):
                if i < NT:
                    B_s0(i, 0)
                if 0 <= i - 1 < NT:
                    B_s1(i - 1)
                if i + 1 < NT:
                    B_loadsg(i + 1)
                if i < NT:
                    B_s0(i, 1)
                if 0 <= i - 2 < NT:
                    B_s2(i - 2)
            P.barrier()

            if stop == "B":
                break
            astate["off"] = A_MARK
            f2 = alloc("f2", 64 * 96, BF16)
            uall = alloc("uall", 32 * 512, BF16)
            bsb = alloc("bsb", 128 * 128, BF16)
            atsb = [alloc("atsb%d" % i, 4 * 2 * 512, BF16) for i in range(2)]
            fbd = [alloc("fbd%d" % i, 8 * 384, BF16) for i in range(2)]
            f2_3 = f2.ap.rearrange("p (a n) -> p a n", a=64)
            bsb3 = bsb.ap.rearrange("p (c n) -> p c n", c=128)
            b_bsb = [P.buf("bsbq%d" % i) for i in range(32)]
            p_b = [pbank(i, "p_b%d" % i) for i in range(2)]
            p_a = [pbank(2 + i, "p_a%d" % i) for i in range(4)]
            p_r = [pbank(6 + i, "p_r%d" % i) for i in range(2)]
            for fb in fbd:
                P.op("pool", lambda e, fb=fb: e.memset(fb.ap, 0.0), writes=[fb.b])
            u_v = u_d.rearrange("(a b) c -> a b c", b=32)
            uall4 = uall.ap.rearrange("p (c g b) -> p g c b", c=128, g=4)
            b_u2 = []
            for i in range(4):
                stg = atsb[i % 2]
                P.op("sp", lambda e, i=i, stg=stg: e.dma_start(
                    out=stg.ap.rearrange("p (b c) -> p b c", b=8), in_=u_v[:, 8 * i:8 * i + 8, :]),
                    writes=[stg.b], dma=stg.b)
                stg4 = stg.ap.rearrange("p (b g c) -> p g c b", b=8, g=4)
                for g in range(4):
                    bq = P.buf("u2_%d_%d" % (i, g))
                    b_u2.append(bq)
                    eng = ("dve", "act", "dve", "act")[g]
                    if eng == "act":
                        P.op("act", lambda e, i=i, g=g, stg4=stg4: e.activation(
                            out=uall4[:, g, :, 8 * i:8 * i + 8], in_=stg4[:, g], func=ACT.Copy),
                            reads=[stg.b], writes=[bq])
                    else:
                        P.op(eng, lambda e, i=i, g=g, stg4=stg4: e.tensor_copy(
                            out=uall4[:, g, :, 8 * i:8 * i + 8], in_=stg4[:, g]),
                            reads=[stg.b], writes=[bq])
            for half in range(2):
                P.op("sp", lambda e, half=half: e.dma_start(
                    out=f2.ap, in_=f2_d[:, 6144 * half:6144 * half + 6144]), writes=[f2.b], dma=f2.b)
                for k in range(32):
                    pb_ = p_b[k % 2]

                    def s1f(e, k=k, pb_=pb_, half=half):
                        i = None
                        for j in range(4):
                            cq = 4 * k + j
                            i = e.matmul(pb_.ap[:, 128 * j:128 * j + 128], uall.ap[:, 128 * cq:128 * cq + 128],
                                         f1.ap[:, 128 * half:128 * half + 128], start=True, stop=True)
                        return i
                    P.op("pe", s1f, reads=b_u2 + [f1.b], writes=[pb_.b])
                    dst = bsb.ap[:, 512 * k:512 * k + 512]
                    if k % 2 == 0:
                        P.op("act", lambda e, dst=dst, pb_=pb_: e.activation(out=dst, in_=pb_.ap, func=ACT.Copy),
                             reads=[pb_.b], writes=[b_bsb[k]])
                    else:
                        P.op("dve", lambda e, dst=dst, pb_=pb_: e.tensor_copy(out=dst, in_=pb_.ap),
                             reads=[pb_.b], writes=[b_bsb[k]])
                for cb in range(4):
                    at_ = atsb[cb % 2]
                    at5 = at_.ap.rearrange("p (j g r s) -> p g r j s", j=16, g=4, r=2)
                    b_at_parts = []
                    for sb in range(2):
                        bb = 2 * cb + sb
                        fb = fbd[bb % 2]
                        fb4 = fb.ap.rearrange("p (l g n) -> p l g n", l=8, g=4)
                        b_fb = [P.buf("fbq%d_%d" % (bb % 2, g)) for g in range(4)]
                        for g in range(4):
                            src = f2_3[32 * g:32 * g + 32, 8 * bb:8 * bb + 8, :]
                            dst = fb4[32 * g:32 * g + 32, :, g, :]
                            eng = ("pool", "act", "act", "act")[g]
                            if eng == "act":
                                P.op("act", lambda e, src=src, dst=dst: e.activation(out=dst, in_=src, func=ACT.Copy),
                                     reads=[f2.b, fb.b], writes=[b_fb[g]])
                            else:
                                P.op(eng, lambda e, src=src, dst=dst: e.tensor_copy(out=dst, in_=src),
                                     reads=[f2.b, fb.b], writes=[b_fb[g]])
                        for pr in range(4):
                            def s2f(e, bb=bb, pr=pr, fb4=fb4):
                                i = None
                                for l2 in range(2):
                                    l = 2 * pr + l2
                                    s1l = 8 * bb + l
                                    for ri in range(2):
                                        off = 32 if ri == 0 else 0
                                        i = e.matmul(p_a[pr].ap[:, 256 * l2:256 * l2 + 256],
                                                     bsb3[:, :, 64 * ri + s1l],
                                                     fb4[:, l, :, off:off + 64],
                                                     start=(ri == 0), stop=(ri == 1))
                                return i
                            P.op("pe", s2f, reads=b_bsb + b_fb, writes=[p_a[pr].b, fb.b])
                            src = p_a[pr].ap
                            j0 = 8 * sb + 2 * pr
                            dst = at_.ap[:, 256 * j0:256 * j0 + 512]
                            b_atp = P.buf("atp")
                            b_at_parts.append(b_atp)
                            if pr % 2 == 0:
                                P.op("act", lambda e, src=src, dst=dst: e.activation(out=dst, in_=src, func=ACT.Copy),
                                     reads=[p_a[pr].b, at_.b], writes=[b_atp])
                            else:
                                P.op("dve", lambda e, src=src, dst=dst: e.tensor_copy(out=dst, in_=src),
                                     reads=[p_a[pr].b, at_.b], writes=[b_atp])
                    s1p0 = 64 * half + 16 * cb
                    for g in range(4):
                        pr_ = p_r[g % 2]

                        def cm(e, g=g, pr_=pr_, at5=at5):
                            e.matmul(pr_.ap, wcs4[:, g, 0, :], at5[:, g, 0, :, :], start=True, stop=False)
                            return e.matmul(pr_.ap, wcs4[:, g, 1, :], at5[:, g, 1, :, :], start=False, stop=True)
                        P.op("pe", cm, reads=b_at_parts + [wcs.b], writes=[pr_.b] + ([at_.b] if g == 3 else []))
                        view = bass.AP(sgT.ap.tensor, sgT3[:, g, s1p0:s1p0 + 1].offset,
                                       [list(sgT.ap.ap[0]), [128, 32], [1, 16]])
                        b_sg = P.buf("sg_%d" % g)
                        P.op("dve", lambda e, view=view, pr_=pr_: e.tensor_tensor(
                            out=view, in0=pr_.ap.rearrange("p (j s) -> p s j", j=16), in1=view, op=ALU.mult),
                            reads=[pr_.b], writes=[b_sg])
            P.barrier()
            if debug and slot == 0 and layer == 0:
                dma(dbg["mixf"][:, :], sgT.ap, sgT, False)
                P.barrier()

            if stop == "C":
                break
            astate["off"] = A_MARK
            xd = [alloc("xd%d" % i, 1024, F32) for i in range(3)]
            p_yd = [[pbank(2 * i + hf, "p_yd%d_%d" % (i, hf)) for hf in range(2)] for i in range(4)]

            def D_load(t):
                dma(xd[t % 3].ap, xsrc[t * 128:(t + 1) * 128, :], xd[t % 3], True)
            D_load(0)
            D_load(1)
            for t in range(NT):
                y_ = xd[t % 3]
                if t + 2 < NT:
                    D_load(t + 2)
                for hf in range(2):
                    py = p_yd[t % 4][hf]

                    def od(e, t=t, hf=hf, py=py):
                        i = None
                        for j in range(8):
                            src = mTa3 if j < 4 else sgT3
                            i = e.matmul(py.ap, src[:, j % 4, 128 * t:128 * t + 128],
                                         w_out3[:, j, 512 * hf:512 * hf + 512],
                                         start=(j == 0), stop=(j == 7))
                        return i
                    P.op("pe", od, reads=[sgT.b, mTa.b, w_out_bf.b], writes=[py.b])
                    P.op("dve", lambda e, hf=hf, y_=y_, py=py: e.tensor_tensor(
                        out=y_.ap[:, 512 * hf:512 * hf + 512], in0=py.ap,
                        in1=y_.ap[:, 512 * hf:512 * hf + 512], op=ALU.add),
                        reads=[py.b, y_.b], writes=[y_.b])
                dma(ydst[128 * t:128 * t + 128, :], y_.ap, y_, False)
            P.barrier()
    P.emit()
    return nc, P


_CACHE = {}


def _host_inputs(x_slots, norm_gain, w_in, q_norm_gain, k_norm_gain, sink_logit, w_fourier, w_out):
    c = _consts()
    gain_d = np.ascontiguousarray(norm_gain.reshape(2, 8, 128).transpose(0, 2, 1)).astype(np.float32)
    qkg = np.concatenate([np.tile(q_norm_gain, (1, 8)), np.tile(k_norm_gain, (1, 2))], axis=1)
    qkg = np.ascontiguousarray(qkg.reshape(2, 1, 640)).astype(np.float32)
    sink = np.ascontiguousarray(sink_logit.reshape(2, 1, 8)).astype(np.float32)
    base = {
        "w_in": np.ascontiguousarray(w_in, dtype=np.float32),
        "w_out": np.ascontiguousarray(w_out, dtype=np.float32),
        "gain_d": gain_d, "qkg": qkg, "sink": sink,
        "w_four": np.ascontiguousarray(w_fourier, dtype=np.float32),
        "ident": c["ident"], "f1": c["f1"], "f2": c["f2"], "cs128": c["cs128"],
        "cosr": c["cosr"], "sinr": c["sinr"], "masks": c["masks"],
    }
    maps = []
    for xs in x_slots:
        m = dict(base)
        m["x"] = xs
        maps.append(m)
    return maps


def kernel(x_prompt, x_sample, norm_gain, w_in, q_norm_gain, k_norm_gain, sink_logit, w_fourier, w_out):
    x_prompt = np.asarray(x_prompt, dtype=np.float32)
    x_sample = np.asarray(x_sample, dtype=np.float32)
    if "nc" not in _CACHE:
        _CACHE["nc"] = build(2, 2)[0]
    nc = _CACHE["nc"]
    x_slots = []
    for c in range(N_CORES):
        s0 = x_sample[c]
        s1 = x_prompt[c] if c < 4 else x_sample[c]
        x_slots.append(np.ascontiguousarray(np.stack([s0, s1], axis=0)))
    maps = _host_inputs(x_slots, np.asarray(norm_gain), np.asarray(w_in), np.asarray(q_norm_gain),
                        np.asarray(k_norm_gain), np.asarray(sink_logit), np.asarray(w_fourier),
                        np.asarray(w_out))
    res = run_bass_kernel_spmd(nc, maps, core_ids=list(range(N_CORES)))
    ys = [np.asarray(r["y"]) for r in res.results]
    y_sample = np.stack([ys[c][0] for c in range(N_CORES)], axis=0).astype(np.float32)
    y_prompt = np.stack([ys[c][1] for c in range(4)], axis=0).astype(np.float32)
    return (y_prompt, y_sample)
```

```python
import numpy as np
import ml_dtypes
from contextlib import ExitStack
import concourse.bass as bass
import concourse.mybir as mybir
from concourse.bass_utils import run_bass_kernel_spmd

F32 = mybir.dt.float32
BF16 = mybir.dt.bfloat16
ACT = mybir.ActivationFunctionType
ALU = mybir.AluOpType
AX = mybir.AxisListType

S = 4096
D = 1024
NT = 32
INW = 2304
EPS = 1e-6
N_CORES = 8


class Buf:
    __slots__ = ("name", "last_w", "readers", "dma_readers", "sem", "cnt")

    def __init__(self, name):
        self.name = name
        self.last_w = None
        self.readers = []
        self.dma_readers = []
        self.sem = None
        self.cnt = 0


class Op:
    __slots__ = ("idx", "eng", "fn", "dma", "dbuf", "dcnt", "deps", "signal",
                 "count", "waits", "barrier", "odeps", "cost")

    def __init__(self):
        self.deps = set()
        self.odeps = set()
        self.cost = 0.5
        self.signal = False
        self.count = None
        self.waits = []
        self.barrier = False
        self.dma = False
        self.dbuf = None
        self.dcnt = None
        self.fn = None


ENGS = ("sp", "act", "dve", "pool", "pe")


class Prog:
    def __init__(self, nc):
        self.nc = nc
        self.ops = []
        self.bufs = []
        self.stack = ExitStack()
        self.dma_since_barrier = []
        self.esem = {}
        self.last_compute = {}

    def buf(self, name):
        b = Buf(name)
        self.bufs.append(b)
        return b

    def op(self, eng, fn, reads=(), writes=(), dma=None):
        o = Op()
        o.idx = len(self.ops)
        o.eng = eng
        o.fn = fn
        if dma is not None:
            o.dma = True
            o.dbuf = dma
            dma.cnt += 16
            o.dcnt = dma.cnt
            self.dma_since_barrier.append(o.idx)
        else:
            self.last_compute[eng] = o.idx
        for r in reads:
            if r.last_w is not None:
                self._dep(o, r.last_w, True)
        for w in writes:
            if w.last_w is not None:
                self._dep(o, w.last_w, True)
            for i in w.readers:
                self._dep(o, i, False)
            for i in w.dma_readers:
                self._dep(o, i, False)
        for r in reads:
            if o.dma:
                r.dma_readers.append(o.idx)
            else:
                r.readers.append(o.idx)
        for w in writes:
            w.last_w = o.idx
            w.readers = []
            w.dma_readers = []
        self.ops.append(o)
        return o

    def _dep(self, o, i, raw):
        p = self.ops[i]
        if p.idx == o.idx:
            return
        o.odeps.add(i)
        if p.dma:
            o.deps.add(i)
            return
        if o.dma:
            p.signal = True
            o.deps.add(i)
            return
        if p.eng == o.eng and p.eng == "pe":
            return
        p.signal = True
        o.deps.add(i)

    def barrier(self):
        deps = set(self.dma_since_barrier)
        for e, i in self.last_compute.items():
            self.ops[i].signal = True
            deps.add(i)
        for e in ENGS:
            o = Op()
            o.idx = len(self.ops)
            o.eng = e
            o.barrier = True
            o.deps = set(deps)
            self.ops.append(o)
        self.dma_since_barrier = []
        self.last_compute = {}
        for b in self.bufs:
            b.last_w = None
            b.readers = []
            b.dma_readers = []

    class _Rec:
        def __init__(self):
            self.calls = []

        def __getattr__(self, name):
            def f(*a, **k):
                out = k.get("out", a[0] if a else None)
                self.calls.append((name, out))
                return self
            return f

        def then_inc(self, *a, **k):
            return self

    @staticmethod
    def _free_elems(ap):
        sh = tuple(ap.shape)
        n = 1
        for d in sh[1:]:
            n *= d
        return n, sh

    def _estimate(self, o):
        rec = Prog._Rec()
        o.fn(rec)
        c = 0.0
        for name, out in rec.calls:
            n, sh = Prog._free_elems(out)
            small = len(sh) >= 3 and sh[-1] <= 32
            if name == "dma_start":
                nbytes = n * sh[0] * (4 if out.dtype == F32 else 2)
                c += nbytes / 170e3
            elif o.eng == "pe":
                c += max(0.056, 0.012 + n * 0.00043)
            elif o.eng == "act":
                c += 0.2 + n * 0.00095
            elif o.eng == "dve":
                c += (0.1 + n * 0.0011) * (2.6 if small else 1.0)
            else:
                c += (0.15 + n * 0.0018) * (1.6 if small else 1.0)
        return c

    def schedule(self, window=48, lat=0.8, dma_lat=2.5):
        queues = {e: [] for e in ENGS}
        seg = []
        segs = []
        for o in self.ops:
            if o.barrier:
                if seg:
                    segs.append((seg, None))
                    seg = []
                segs.append((None, o))
            else:
                seg.append(o)
        if seg:
            segs.append((seg, None))
        self.sim_time = 0.0
        for ops, bar in segs:
            if bar is not None:
                queues[bar.eng].append(bar)
                continue
            for o in ops:
                o.cost = self._estimate(o)
            inseg = {o.idx for o in ops}
            pend = {e: [o for o in ops if o.eng == e] for e in ENGS}
            free = {e: 0.0 for e in ENGS}
            dma_free = 0.0
            done = {}
            nleft = len(ops)
            while nleft:
                best = None
                for e in ENGS:
                    cand = pend[e][:window]
                    for o in cand:
                        st = free[e]
                        ok = True
                        for d in o.odeps:
                            if d in inseg:
                                if d not in done:
                                    ok = False
                                    break
                                p = self.ops[d]
                                need_sync = (d in o.deps)
                                t = done[d] + (lat if need_sync else 0.0)
                                if t > st:
                                    st = t
                        if not ok:
                            continue
                        if best is None or st < best[0] - 1e-9 or (abs(st - best[0]) <= 1e-9 and o.idx < best[2].idx):
                            best = (st, e, o)
                        if st <= free[e] + 1e-9:
                            break
                assert best is not None, "scheduler deadlock"
                st, e, o = best
                if o.dma:
                    free[e] = st + 0.06
                    beg = max(st, dma_free)
                    dma_free = beg + o.cost
                    done[o.idx] = dma_free + dma_lat
                else:
                    free[e] = st + o.cost
                    done[o.idx] = free[e]
                pend[e].remove(o)
                queues[e].append(o)
                nleft -= 1
            self.sim_time += max(done.values()) if done else 0.0
        self.queues = queues

    def emit(self, sched=True):
        nc = self.nc
        st = self.stack
        for e in ENGS:
            self.esem[e] = st.enter_context(nc.semaphore("es_" + e))
        nsem = 0
        for b in self.bufs:
            if b.cnt > 0:
                b.sem = st.enter_context(nc.semaphore("ds%d_%s" % (nsem, b.name)))
                nsem += 1
        self.n_dma_sems = nsem
        if sched:
            self.schedule()
        else:
            self.queues = {e: [o for o in self.ops if o.eng == e] for e in ENGS}
        by_eng = self.queues
        cnt = {e: 0 for e in ENGS}
        for e in ENGS:
            for o in by_eng[e]:
                if o.barrier or o.dma:
                    continue
                if o.signal:
                    cnt[e] += 1
                    o.count = cnt[e]
        self.sig_counts = dict(cnt)
        for e in ENGS:
            w = {}
            for o in by_eng[e]:
                need = {}
                for i in o.deps:
                    p = self.ops[i]
                    if p.dma:
                        key = ("d", id(p.dbuf))
                        sem, val = p.dbuf.sem, p.dcnt
                    else:
                        key = ("e", p.eng)
                        sem, val = self.esem[p.eng], p.count
                    if key not in need or need[key][1] < val:
                        need[key] = (sem, val)
                for key, (sem, val) in need.items():
                    if w.get(key, 0) >= val:
                        continue
                    w[key] = val
                    o.waits.append((sem, val))

        def replay(name, e):
            for o in by_eng[name]:
                for sem, val in o.waits:
                    e.wait_ge(sem, val)
                if o.barrier:
                    continue
                inst = o.fn(e)
                if o.dma:
                    inst.then_inc(o.dbuf.sem, 16)
                elif o.signal:
                    inst.then_inc(self.esem[name], 1)

        with nc.Block() as block:
            @block.sync
            def _(e):
                replay("sp", e)

            @block.scalar
            def _(e):
                replay("act", e)

            @block.vector
            def _(e):
                replay("dve", e)

            @block.gpsimd
            def _(e):
                replay("pool", e)

            @block.tensor
            def _(e):
                replay("pe", e)
        st.close()


class T:
    __slots__ = ("ap", "b")

    def __init__(self, ap, b):
        self.ap = ap
        self.b = b


def _consts():
    bf = ml_dtypes.bfloat16
    c = {}
    c["ident"] = np.eye(128, dtype=np.float32).astype(bf)
    s1 = np.arange(128, dtype=np.float64)
    ang = 2 * np.pi * np.outer(s1, s1) / 128.0
    f1 = np.stack([np.cos(ang), -np.sin(ang)], axis=1) / 64.0
    f1 = f1.reshape(128, 2, 2, 64).transpose(0, 2, 1, 3)
    c["f1"] = np.ascontiguousarray(f1).reshape(128, 256).astype(np.float32).astype(bf)
    s2 = np.arange(32, dtype=np.float64)
    s1p = np.arange(128, dtype=np.float64)
    s2p = np.arange(32, dtype=np.float64)
    th = 2 * np.pi * (s2[:, None, None] * s1p[None, :, None] / 4096.0
                      + s2[:, None, None] * s2p[None, None, :] / 32.0)
    tr, ti = np.cos(th), -np.sin(th)
    f2 = np.concatenate([-ti, tr, ti], axis=2)
    f2 = f2.reshape(32, 128 * 96)
    c["f2"] = np.tile(f2, (4, 1)).astype(np.float32).astype(bf)
    a2 = 2 * np.pi * np.outer(s1, s1) / 128.0
    c["cs128"] = (np.concatenate([np.cos(a2), np.sin(a2)], axis=1) / np.sqrt(128.0)
                  ).astype(np.float32).astype(bf)
    half = 32
    inv = (1.0 / (10000.0 ** (np.arange(half, dtype=np.float32) / half))).astype(np.float32)
    pos = np.arange(S, dtype=np.float32)
    angr = (pos[:, None] * inv[None, :]).astype(np.float32)
    cosr = np.cos(angr).astype(np.float32).reshape(NT, 128, half).transpose(1, 0, 2)
    sinr = np.sin(angr).astype(np.float32).reshape(NT, 128, half).transpose(1, 0, 2)
    c["cosr"] = np.ascontiguousarray(cosr).reshape(128, NT * half)
    c["sinr"] = np.ascontiguousarray(sinr).reshape(128, NT * half)
    j = np.arange(128)[:, None]
    i = np.arange(128)[None, :]
    ml = (j >= i).astype(np.float32)
    mr = (j <= i).astype(np.float32)
    c["masks"] = np.concatenate([np.tile(ml, (1, 4)), np.tile(mr, (1, 4))], axis=1).astype(bf)
    return c


def build(n_slots=2, n_layers=2, debug=False, stop=None):
    nc = bass.Bass("TRN2", target_bir_lowering=False)
    P = Prog(nc)
    st = P.stack

    def din(name, shape, dt=F32):
        return nc.dram_tensor(name, list(shape), dt, kind="ExternalInput").ap()

    x_d = din("x", [n_slots, S, D])
    w_in_d = din("w_in", [2, D, INW])
    w_out_d = din("w_out", [2, D, D])
    gd_d = din("gain_d", [2, 128, 8])
    qkg_d = din("qkg", [2, 1, 640])
    sink_d = din("sink", [2, 1, 8])
    wf_d = din("w_four", [2, 4, 128, 128])
    ident_d = din("ident", [128, 128], BF16)
    f1_d = din("f1", [128, 256], BF16)
    f2_d = din("f2", [128, 12288], BF16)
    cs_d = din("cs128", [128, 256], BF16)
    cos_d = din("cosr", [128, NT * 32])
    sin_d = din("sinr", [128, NT * 32])
    masks_d = din("masks", [128, 1024], BF16)
    y_d = nc.dram_tensor("y", [n_slots, S, D], F32, kind="ExternalOutput").ap()
    u_d = nc.dram_tensor("u_scr", [S, 512], BF16, kind="Internal").ap()
    sga_d = nc.dram_tensor("sga_scr", [S, 512], BF16, kind="Internal").ap()
    y1_d = nc.dram_tensor("y1_scr", [n_slots, S, D], F32, kind="Internal").ap()
    wc_d = nc.dram_tensor("w_in_cache", [128, 8 * INW], BF16, kind="Internal").ap()
    dbg = {}
    if debug:
        dbg["qkt"] = nc.dram_tensor("dbg_qkt", [128, 6 * S], BF16, kind="ExternalOutput").ap()
        dbg["vall"] = nc.dram_tensor("dbg_vall", [128, NT * 130], BF16, kind="ExternalOutput").ap()
        dbg["sgt"] = nc.dram_tensor("dbg_sgt", [128, 4 * S], BF16, kind="ExternalOutput").ap()
        dbg["u"] = nc.dram_tensor("dbg_u", [S, 512], BF16, kind="ExternalOutput").ap()
        dbg["sga"] = nc.dram_tensor("dbg_sga", [S, 512], BF16, kind="ExternalOutput").ap()
        dbg["ypart"] = nc.dram_tensor("dbg_ypart", [S, D], F32, kind="ExternalOutput").ap()
        dbg["mixf"] = nc.dram_tensor("dbg_mixf", [128, 4 * S], BF16, kind="ExternalOutput").ap()

    ARENA_BYTES = 207 * 1024
    arena = st.enter_context(nc.sbuf_tensor("arena", [128, ARENA_BYTES // 2], BF16))
    astate = {"off": 0}

    def alloc(name, n, dt):
        nbytes = n * (2 if dt == BF16 else 4)
        off = (astate["off"] + 63) // 64 * 64
        assert off + nbytes <= ARENA_BYTES, (name, off, nbytes)
        astate["off"] = off + nbytes
        if dt == BF16:
            ap = arena[:, off // 2: off // 2 + n]
        else:
            ap = arena[:, off // 2: off // 2 + 2 * n].bitcast(F32)
        return T(ap, P.buf(name))

    psum = [st.enter_context(nc.psum_tensor("ps%d" % i, [128, 512], F32)) for i in range(8)]

    def pbank(i, name):
        return T(psum[i][:, :], P.buf(name))

    def pbf(t):
        return t.ap.bitcast(BF16)

    ident = alloc("ident", 128, BF16)
    f1 = alloc("f1", 256, BF16)
    cs128 = alloc("cs128", 256, BF16)
    masks = alloc("masks", 1024, BF16)
    cosr = alloc("cosr", NT * 32, F32)
    sinr = alloc("sinr", NT * 32, F32)
    g640 = alloc("g640", 640, F32)
    gd = alloc("gd", 8, F32)
    sk = alloc("sk", 8, F32)
    esink = alloc("esink", 8, F32)
    wcs = alloc("wcs", 4 * 2 * 128, BF16)
    w_out_bf = alloc("w_out_bf", 8 * 1024, BF16)
    sgT = alloc("sgT", 4 * S, BF16)
    P_MARK = astate["off"]

    def dma(out, in_, t, write):
        if write:
            P.op("sp", lambda e: e.dma_start(out=out, in_=in_), writes=[t.b], dma=t.b)
        else:
            P.op("sp", lambda e: e.dma_start(out=out, in_=in_), reads=[t.b], dma=t.b)

    dma(ident.ap, ident_d[:, :], ident, True)
    dma(f1.ap, f1_d[:, :], f1, True)
    dma(cs128.ap, cs_d[:, :], cs128, True)
    dma(masks.ap, masks_d[:, :], masks, True)
    dma(cosr.ap, cos_d[:, :], cosr, True)
    dma(sinr.ap, sin_d[:, :], sinr, True)

    w_out3 = w_out_bf.ap.rearrange("p (j n) -> p j n", j=8)
    sgT3 = sgT.ap.rearrange("p (j s) -> p j s", j=4)

    wcs4 = wcs.ap.rearrange("p (g r d) -> p g r d", g=4, r=2)

    for layer in range(n_layers):
        for slot in range(n_slots):
            xsrc = x_d[slot] if layer == 0 else y1_d[slot]
            ydst = y1_d[slot] if layer < n_layers - 1 else y_d[slot]
            last = (slot == n_slots - 1 and layer == n_layers - 1)

            astate["off"] = P_MARK
            w_in_bf = alloc("w_in_bf", 8 * INW, BF16)
            w_in3 = w_in_bf.ap.rearrange("p (k n) -> p k n", k=8)
            A_MARK = astate["off"]
            mTa = T(w_in_bf.ap[:, 0:4 * S], P.buf("mTa"))
            mTa3 = mTa.ap.rearrange("p (j s) -> p j s", j=4)
            ws = [alloc("ws%d" % i, INW, F32) for i in range(2)]
            wf32 = alloc("wf32", 512, F32)
            wfb = alloc("wfb", 512, BF16)
            pw0 = pbank(0, "pw0")
            pw1 = pbank(1, "pw1")

            if slot == 0:
                dma(gd.ap, gd_d[layer], gd, True)
                dma(g640.ap, qkg_d[layer, 0:1, :].partition_broadcast(128), g640, True)
                dma(sk.ap, sink_d[layer, 0:1, :].partition_broadcast(128), sk, True)
                P.op("act", lambda e: e.activation(out=esink.ap, in_=sk.ap, func=ACT.Exp),
                     reads=[sk.b], writes=[esink.b])
                for kc in range(8):
                    w = ws[kc % 2]
                    dma(w.ap, w_in_d[layer, kc * 128:(kc + 1) * 128, :], w, True)
                    if kc % 2 == 0:
                        P.op("dve", lambda e, w=w, kc=kc: e.tensor_scalar(
                            out=w_in3[:, kc, :], in0=w.ap, scalar1=gd.ap[:, kc:kc + 1], scalar2=None,
                            op0=ALU.mult), reads=[w.b, gd.b], writes=[w_in_bf.b])
                    else:
                        P.op("act", lambda e, w=w, kc=kc: e.activation(
                            out=w_in3[:, kc, :], in_=w.ap, func=ACT.Copy, scale=gd.ap[:, kc:kc + 1]),
                            reads=[w.b, gd.b], writes=[w_in_bf.b])
                dma(wf32.ap.rearrange("p (g d) -> p g d", g=4),
                    wf_d[layer].rearrange("g c d -> c g d"), wf32, True)
                P.op("dve", lambda e: e.tensor_copy(out=wfb.ap, in_=wf32.ap), reads=[wf32.b], writes=[wfb.b])

                def wfmm(e):
                    i = None
                    for g in range(4):
                        e.matmul(pw0.ap[:, 128 * g:128 * g + 128], cs128.ap[:, 0:128],
                                 wfb.ap[:, 128 * g:128 * g + 128], start=True, stop=True)
                        i = e.matmul(pw1.ap[:, 128 * g:128 * g + 128], cs128.ap[:, 128:256],
                                     wfb.ap[:, 128 * g:128 * g + 128], start=True, stop=True)
                    return i
                P.op("pe", wfmm, reads=[cs128.b, wfb.b], writes=[pw0.b, pw1.b])
                P.op("dve", lambda e: e.tensor_copy(out=wcs4[:, :, 0, :],
                                                    in_=pw0.ap.rearrange("p (g d) -> p g d", g=4)),
                     reads=[pw0.b], writes=[wcs.b])
                P.op("act", lambda e: e.activation(out=wcs4[:, :, 1, :],
                                                   in_=pw1.ap.rearrange("p (g d) -> p g d", g=4),
                                                   func=ACT.Copy),
                     reads=[pw1.b], writes=[wcs.b])

                for i in range(4):
                    P.op("sp", lambda e, i=i: e.dma_start(
                        out=wc_d[:, 2 * INW * i:2 * INW * (i + 1)],
                        in_=w_in_bf.ap[:, 2 * INW * i:2 * INW * (i + 1)]), reads=[w_in_bf.b], dma=w_in_bf.b)
            else:
                for i in range(4):
                    P.op("sp", lambda e, i=i: e.dma_start(
                        out=w_in_bf.ap[:, 2 * INW * i:2 * INW * (i + 1)],
                        in_=wc_d[:, 2 * INW * i:2 * INW * (i + 1)]), writes=[w_in_bf.b], dma=w_in_bf.b)
            if slot == 0:
                P.barrier()
            if stop == "W":
                break

            astate["off"] = A_MARK
            qkt = alloc("qkt", 6 * S, BF16)
            qkt3 = qkt.ap.rearrange("p (j s) -> p j s", j=6)
            vall = alloc("vall", NT * 130, BF16)
            vall4 = vall.ap.rearrange("p (t k d) -> p t k d", t=NT, k=2)
            B_MARK = astate["off"]
            xt = [alloc("xt%d" % i, 1024, F32) for i in range(3)]
            hb = [alloc("hb%d" % i, 1024, BF16) for i in range(2)]
            hT = [alloc("hT%d" % i, 1024, BF16) for i in range(2)]
            stt = [alloc("stt%d" % i, 16, F32) for i in range(2)]
            qk32 = [alloc("qk32_%d" % i, 640, F32) for i in range(2)]
            sq = [alloc("sq%d" % i, 640, F32) for i in range(2)]
            hst = [alloc("hst%d" % i, 32, F32) for i in range(2)]
            t1 = alloc("t1", 640, F32)
            rm = [alloc("rm%d" % i, 320, F32) for i in range(4)]
            qrot = [alloc("qrot%d" % i, 640, BF16) for i in range(2)]
            ea = [alloc("ea%d" % i, 512, F32) for i in range(2)]
            sga = [alloc("sga%d" % i, 512, BF16) for i in range(1)] * 2
            ub = [alloc("ub%d" % i, 512, BF16) for i in range(1)] * 2
            sgf = [alloc("sgf%d" % i, 512, BF16) for i in range(3)]
            p_trh = pbank(0, "p_trh")
            p_q = pbank(1, "p_q")
            p_kv = pbank(2, "p_kv")
            p_ga = pbank(3, "p_ga")
            p_u = pbank(4, "p_u")
            p_gf = pbank(5, "p_gf")
            p_trq = pbank(6, "p_trq")
            p_trg = pbank(7, "p_trg")

            qr_halves = {}
            P.op("pool", lambda e: e.memset(vall.ap, 1.0), writes=[vall.b])
            P.op("pool", lambda e: e.memset(qkt.ap[:, 4 * S:6 * S], 0.0), writes=[qkt.b])

            def A_load(t):
                dma(xt[t % 3].ap, xsrc[t * 128:(t + 1) * 128, :], xt[t % 3], True)

            def A_pre1(t):
                x_, s_, h_ = xt[t % 3], stt[t % 2], hb[t % 2]
                P.op("act", lambda e: e.activation(out=h_.ap, in_=x_.ap, func=ACT.Square,
                                                   accum_out=s_.ap[:, 0:1]),
                     reads=[x_.b], writes=[h_.b, s_.b])
                P.op("dve", lambda e: e.tensor_scalar(out=s_.ap[:, 1:2], in0=s_.ap[:, 0:1], scalar1=1.0 / D,
                                                      scalar2=EPS, op0=ALU.mult, op1=ALU.add),
                     reads=[s_.b], writes=[s_.b])
                P.op("act", lambda e: e.activation(out=s_.ap[:, 2:3], in_=s_.ap[:, 1:2], func=ACT.Ln),
                     reads=[s_.b], writes=[s_.b])
                P.op("act", lambda e: e.activation(out=s_.ap[:, 3:4], in_=s_.ap[:, 2:3], func=ACT.Exp, scale=-0.5),
                     reads=[s_.b], writes=[s_.b])
                P.op("pool", lambda e: e.tensor_tensor(out=h_.ap, in0=x_.ap,
                                                       in1=s_.ap[:, 3:4].to_broadcast([128, 1024]), op=ALU.mult),
                     reads=[x_.b, s_.b], writes=[h_.b])

            def A_pre2(t):
                h_, hT_ = hb[t % 2], hT[t % 2]

                def trh(e):
                    i = None
                    pb = pbf(p_trh)
                    for k in range(8):
                        i = e.transpose(pb[:, 128 * k:128 * k + 128], h_.ap[:, 128 * k:128 * k + 128], ident.ap)
                    return i
                P.op("pe", trh, reads=[h_.b, ident.b], writes=[p_trh.b])
                P.op("dve", lambda e: e.tensor_copy(out=hT_.ap, in_=pbf(p_trh)), reads=[p_trh.b], writes=[hT_.b])

            def gate(pt, ea_, out_):
                P.op("act", lambda e: e.activation(out=ea_.ap, in_=pt.ap, func=ACT.Exp, scale=-1.0),
                     reads=[pt.b], writes=[ea_.b])
                P.op("act", lambda e: e.activation(out=ea_.ap, in_=ea_.ap, func=ACT.Ln, bias=1.0),
                     reads=[ea_.b], writes=[ea_.b])
                P.op("act", lambda e: e.activation(out=ea_.ap, in_=ea_.ap, func=ACT.Exp, scale=-1.0),
                     reads=[ea_.b], writes=[ea_.b])
                P.op("dve", lambda e: e.tensor_tensor(out=out_.ap, in0=pt.ap, in1=ea_.ap, op=ALU.mult),
                     reads=[pt.b, ea_.b], writes=[out_.b])

            def A_front(t, part):
                hT_ = hT[t % 2]
                hT3 = hT_.ap.rearrange("p (k n) -> p k n", k=8)
                q32, sga_, sgf_, ub_ = qk32[t % 2], sga[t % 2], sgf[t % 3], ub[t % 2]

                def proj(pt, c0, n):
                    def f(e):
                        i = None
                        for k in range(8):
                            i = e.matmul(pt.ap[:, 0:n], hT3[:, k, :], w_in3[:, k, c0:c0 + n],
                                         start=(k == 0), stop=(k == 7))
                        return i
                    P.op("pe", f, reads=[hT_.b, w_in_bf.b], writes=[pt.b])
                if part == 1:
                    proj(p_ga, 768, 512)
                    gate(p_ga, ea[0], sga_)
                    dma(sga_d[128 * t:128 * t + 128, :], sga_.ap, sga_, False)
                    proj(p_u, 1280, 512)
                    P.op("dve", lambda e: e.tensor_copy(out=ub_.ap, in_=p_u.ap),
                         reads=[p_u.b], writes=[ub_.b])
                    dma(u_d[128 * t:128 * t + 128, :], ub_.ap, ub_, False)
                    proj(p_gf, 1792, 512)
                    gate(p_gf, ea[1], sgf_)
                    return
                sq_ = sq[t % 2]
                proj(p_q, 0, 512)
                P.op("dve", lambda e: e.tensor_tensor(out=q32.ap[:, 0:512], in0=p_q.ap, in1=g640.ap[:, 0:512],
                                                      op=ALU.mult),
                     reads=[p_q.b, g640.b], writes=[q32.b])
                P.op("act", lambda e: e.activation(out=sq_.ap[:, 0:512], in_=p_q.ap, func=ACT.Square),
                     reads=[p_q.b, q32.b], writes=[sq_.b])
                proj(p_kv, 512, 256)
                P.op("dve", lambda e: e.tensor_tensor(out=q32.ap[:, 512:640], in0=p_kv.ap[:, 0:128],
                                                      in1=g640.ap[:, 512:640], op=ALU.mult),
                     reads=[p_kv.b, g640.b], writes=[q32.b])
                P.op("act", lambda e: e.activation(out=sq_.ap[:, 512:640], in_=p_kv.ap[:, 0:128], func=ACT.Square),
                     reads=[p_kv.b, q32.b], writes=[sq_.b])
                P.op("act", lambda e: e.activation(
                    out=vall4[:, t, :, 0:64], in_=p_kv.ap[:, 128:256].rearrange("p (k d) -> p k d", k=2),
                    func=ACT.Copy), reads=[p_kv.b, q32.b], writes=[vall.b])

            def A_back1(t):
                q32, hs_, qr_, sq_ = qk32[t % 2], hst[t % 2], qrot[t % 2], sq[t % 2]
                P.op("dve", lambda e: e.tensor_reduce(
                    out=hs_.ap[:, 0:10], in_=sq_.ap.rearrange("p (h d) -> p h d", h=10), axis=AX.X, op=ALU.add),
                    reads=[sq_.b], writes=[hs_.b])
                P.op("dve", lambda e: e.tensor_scalar(
                    out=hs_.ap[:, 10:20], in0=hs_.ap[:, 0:10], scalar1=1.0 / 64, scalar2=EPS,
                    op0=ALU.mult, op1=ALU.add), reads=[hs_.b], writes=[hs_.b])
                P.op("act", lambda e: e.activation(out=hs_.ap[:, 0:10], in_=hs_.ap[:, 10:20], func=ACT.Ln),
                     reads=[hs_.b], writes=[hs_.b])
                P.op("act", lambda e: e.activation(out=hs_.ap[:, 20:30], in_=hs_.ap[:, 0:10], func=ACT.Exp,
                                                   scale=-0.5), reads=[hs_.b], writes=[hs_.b])
                t1_4 = t1.ap[:, 0:512].rearrange("p (hh kv d) -> p kv hh d", hh=4, kv=2)
                q4 = q32.ap[:, 0:512].rearrange("p (kv hh d) -> p kv hh d", kv=2, hh=4)

                def qscale(e):
                    rq = hs_.ap[:, 20:28].rearrange("p (kv hh) -> p kv hh", kv=2)
                    for kv in range(2):
                        e.tensor_tensor(out=t1_4[:, kv], in0=q4[:, kv],
                                        in1=rq[:, kv].unsqueeze(2).to_broadcast([128, 4, 64]), op=ALU.mult)
                    return e.tensor_tensor(
                        out=t1.ap[:, 512:640].rearrange("p (h d) -> p h d", h=2),
                        in0=q32.ap[:, 512:640].rearrange("p (h d) -> p h d", h=2),
                        in1=hs_.ap[:, 28:30].unsqueeze(2).to_broadcast([128, 2, 64]), op=ALU.mult)
                P.op("dve", qscale, reads=[q32.b, hs_.b], writes=[t1.b])
                t13 = t1.ap.rearrange("p (h d) -> p h d", h=10)
                ta, tb = t13[:, :, 0:32], t13[:, :, 32:64]
                cosb = cosr.ap[:, 32 * t:32 * t + 32].unsqueeze(1).to_broadcast([128, 10, 32])
                sinb = sinr.ap[:, 32 * t:32 * t + 32].unsqueeze(1).to_broadcast([128, 10, 32])
                rm3 = [r.ap.rearrange("p (h d) -> p h d", h=10) for r in rm]
                qr3 = qr_.ap.rearrange("p (h d) -> p h d", h=10)
                b_qa, b_qb = P.buf("qra"), P.buf("qrb")

                def rope_a(e):
                    e.tensor_tensor(out=rm3[0], in0=ta, in1=cosb, op=ALU.mult)
                    e.tensor_tensor(out=rm3[1], in0=tb, in1=sinb, op=ALU.mult)
                    return e
                P.op("dve", lambda e: e.tensor_tensor(out=rm3[0], in0=ta, in1=cosb, op=ALU.mult),
                     reads=[t1.b, cosr.b], writes=[rm[0].b])
                P.op("dve", lambda e: e.tensor_tensor(out=rm3[1], in0=tb, in1=sinb, op=ALU.mult),
                     reads=[t1.b, sinr.b], writes=[rm[1].b])
                P.op("pool", lambda e: e.tensor_tensor(out=rm3[2], in0=tb, in1=cosb, op=ALU.mult),
                     reads=[t1.b, cosr.b], writes=[rm[2].b])
                P.op("pool", lambda e: e.tensor_tensor(out=rm3[3], in0=ta, in1=sinb, op=ALU.mult),
                     reads=[t1.b, sinr.b], writes=[rm[3].b])
                P.op("dve", lambda e: e.tensor_tensor(out=qr3[:, :, 0:32], in0=rm3[0], in1=rm3[1],
                                                      op=ALU.subtract),
                     reads=[rm[0].b, rm[1].b, qr_.b], writes=[b_qa])
                P.op("pool", lambda e: e.tensor_tensor(out=qr3[:, :, 32:64], in0=rm3[2], in1=rm3[3], op=ALU.add),
                     reads=[rm[2].b, rm[3].b, qr_.b], writes=[b_qb])
                qr_halves[t % 2] = (b_qa, b_qb)

            def A_back2(t):
                qr_, sgf_ = qrot[t % 2], sgf[t % 3]

                def trq(e):
                    i = None
                    pb = pbf(p_trq)
                    for j in range(5):
                        i = e.transpose(pb[:, 128 * j:128 * j + 128], qr_.ap[:, 128 * j:128 * j + 128], ident.ap)
                    return i
                P.op("pe", trq, reads=list(qr_halves[t % 2]) + [ident.b], writes=[p_trq.b, qr_.b])
                def qcopy(e):
                    pb = pbf(p_trq)
                    e.activation(out=qkt3[:, 0:4, 128 * t:128 * t + 128],
                                 in_=pb[:, 0:512].rearrange("p (j n) -> p j n", j=4), func=ACT.Copy)
                    e.activation(out=qkt3[0:64, 4, 128 * t:128 * t + 128], in_=pb[0:64, 512:640], func=ACT.Copy)
                    return e.activation(out=qkt3[64:128, 5, 128 * t:128 * t + 128], in_=pb[64:128, 512:640],
                                        func=ACT.Copy)
                P.op("act", qcopy, reads=[p_trq.b], writes=[qkt.b])

                def trg(e):
                    i = None
                    pb = pbf(p_trg)
                    for j in range(4):
                        i = e.transpose(pb[:, 128 * j:128 * j + 128], sgf_.ap[:, 128 * j:128 * j + 128], ident.ap)
                    return i
                P.op("pe", trg, reads=[sgf_.b, ident.b], writes=[p_trg.b])
                P.op("act", lambda e: e.activation(
                    out=sgT3[:, :, 128 * t:128 * t + 128],
                    in_=pbf(p_trg)[:, 0:512].rearrange("p (j n) -> p j n", j=4), func=ACT.Copy),
                    reads=[p_trg.b], writes=[sgT.b])

            A_load(0)
            A_load(1)
            A_load(2)
            A_pre1(0)
            A_pre1(1)
            A_pre2(0)
            for i in range(NT + 2):
                if i + 3 < NT:
                    A_load(i + 3)
                if i < NT:
                    A_front(i, 0)
                if i + 2 < NT:
                    A_pre1(i + 2)
                if i + 1 < NT:
                    A_pre2(i + 1)
                if i < NT:
                    A_front(i, 1)
                if 0 <= i - 1 < NT:
                    A_back1(i - 1)
                if 0 <= i - 2 < NT:
                    A_back2(i - 2)
            P.barrier()
            if debug and slot == 0 and layer == 0:
                dma(dbg["qkt"][:, :], qkt.ap, qkt, False)
                dma(dbg["vall"][:, :], vall.ap, vall, False)
                dma(dbg["sgt"][:, :], sgT.ap, sgT, False)
                P.op("sp", lambda e: e.dma_start(out=dbg["u"][:, :], in_=u_d[:, :]), reads=[], dma=ub[0].b)
                P.op("sp", lambda e: e.dma_start(out=dbg["sga"][:, :], in_=sga_d[:, :]), reads=[], dma=ub[1].b)
                P.barrier()

            if stop == "A":
                break
            astate["off"] = B_MARK
            sgl = [alloc("sgl%d" % i, 512, BF16) for i in range(2)]
            pt_ = [[[alloc("pt%d_%d_%d" % (b_, kv, i), 512, BF16) for i in range(3)] for kv in range(2)]
                   for b_ in range(2)]
            dn = alloc("dn", 16, F32)
            on = alloc("on", 512, F32)
            mx = [alloc("mx%d" % i, 512, BF16) for i in range(2)]
            p_s = [pbank(i, "p_s%d" % i) for i in range(3)]
            p_o = [pbank(3 + i, "p_o%d" % i) for i in range(2)]
            p_trm = [pbank(5 + i, "p_trm%d" % i) for i in range(2)]

            if slot == 0:
                wo_st = [alloc("wo_st%d" % i, 1024, F32) for i in range(2)]
                for jc in range(8):
                    w = wo_st[jc % 2]
                    dma(w.ap, w_out_d[layer, jc * 128:(jc + 1) * 128, :], w, True)
                    P.op("pool", lambda e, w=w, jc=jc: e.tensor_copy(out=w_out3[:, jc, :], in_=w.ap),
                         reads=[w.b], writes=[w_out_bf.b])

            def kbs_of(n):
                return [kb for kb in (n - 1, n, n + 1) if 0 <= kb < NT]

            def B_s0(n, kv):
                lo = 64 * kv
                for i, kb in enumerate(kbs_of(n)):
                    P.op("pe", lambda e, i=i, kb=kb: e.matmul(
                        p_s[i].ap, qkt3[:, 4 + kv, 128 * kb:128 * kb + 128],
                        qkt3[:, 0:4, 128 * n:128 * n + 128], start=True, stop=True),
                        reads=[qkt.b], writes=[p_s[i].b])
                    pti = pt_[n % 2][kv][i]
                    P.op("act", lambda e, i=i, pti=pti: e.activation(
                        out=pti.ap, in_=p_s[i].ap, func=ACT.Exp, scale=0.125),
                        reads=[p_s[i].b], writes=[pti.b])
                    if kb != n:
                        mk = masks.ap[:, 0:512] if kb == n - 1 else masks.ap[:, 512:1024]
                        P.op("dve", lambda e, pti=pti, mk=mk: e.tensor_tensor(
                            out=pti.ap, in0=pti.ap, in1=mk, op=ALU.mult),
                            reads=[pti.b, masks.b], writes=[pti.b])

            def B_s1(n):
                kbs = kbs_of(n)
                sg_, mx_ = sgl[n % 2], mx[n % 2]
                for kv in range(2):
                    def pv(e, kv=kv):
                        ins = None
                        for hh in range(4):
                            for i, kb in enumerate(kbs):
                                ins = e.matmul(p_o[kv].ap[:, 65 * hh:65 * hh + 65],
                                               pt_[n % 2][kv][i].ap[:, 128 * hh:128 * hh + 128],
                                               vall4[:, kb, kv, :], start=(i == 0), stop=(i == len(kbs) - 1))
                        return ins
                    P.op("pe", pv, reads=[pt_[n % 2][kv][i].b for i in range(len(kbs))] + [vall.b],
                         writes=[p_o[kv].b])
                for kv in range(2):
                    po3 = p_o[kv].ap[:, 0:260].rearrange("p (h d) -> p h d", h=4)
                    P.op("dve", lambda e, po3=po3, kv=kv: e.tensor_tensor(
                        out=dn.ap[:, 4 * kv:4 * kv + 4].unsqueeze(2), in0=po3[:, :, 64:65],
                        in1=esink.ap[:, 4 * kv:4 * kv + 4].unsqueeze(2), op=ALU.add),
                        reads=[p_o[kv].b, esink.b], writes=[dn.b])
                P.op("dve", lambda e: e.reciprocal(out=dn.ap[:, 8:16], in_=dn.ap[:, 0:8]),
                     reads=[dn.b], writes=[dn.b])
                on3 = on.ap.rearrange("p (h d) -> p h d", h=8)
                for kv in range(2):
                    po3 = p_o[kv].ap[:, 0:260].rearrange("p (h d) -> p h d", h=4)
                    P.op("dve", lambda e, po3=po3, kv=kv: e.tensor_tensor(
                        out=on3[:, 4 * kv:4 * kv + 4, :], in0=po3[:, :, 0:64],
                        in1=dn.ap[:, 8 + 4 * kv:12 + 4 * kv].unsqueeze(2).to_broadcast([128, 4, 64]),
                        op=ALU.mult), reads=[p_o[kv].b, dn.b], writes=[on.b])
                P.op("dve", lambda e: e.tensor_tensor(out=mx_.ap, in0=on.ap, in1=sg_.ap, op=ALU.mult),
                     reads=[on.b, sg_.b], writes=[mx_.b])

            def B_s2(n):
                mx_ = mx[n % 2]
                ptm = p_trm[n % 2]

                def trm(e):
                    i = None
                    pb = pbf(ptm)
                    for j in range(4):
                        i = e.transpose(pb[:, 128 * j:128 * j + 128], mx_.ap[:, 128 * j:128 * j + 128], ident.ap)
                    return i
                P.op("pe", trm, reads=[mx_.b, ident.b], writes=[ptm.b])
                P.op("act", lambda e: e.activation(
                    out=mTa3[:, :, 128 * n:128 * n + 128],
                    in_=pbf(ptm)[:, 0:512].rearrange("p (j n) -> p j n", j=4), func=ACT.Copy),
                    reads=[ptm.b], writes=[mTa.b])

            def B_loadsg(n):
                dma(sgl[n % 2].ap, sga_d[n * 128:(n + 1) * 128, :], sgl[n % 2], True)

            B_loadsg(0)
            for i in range(NT + 2):
                if i < NT:
                    B_s0(i, 0)
                if 0 <= i - 1 < NT:
                    B_s1(i - 1)
                if i + 1 < NT:
                    B_loadsg(i + 1)
                if i < NT:
                    B_s0(i, 1)
                if 0 <= i - 2 < NT:
                    B_s2(i - 2)
            P.barrier()

            if stop == "B":
                break
            astate["off"] = A_MARK
            f2 = alloc("f2", 64 * 96, BF16)
            uall = alloc("uall", 32 * 512, BF16)
            bsb = alloc("bsb", 128 * 128, BF16)
            atsb = [alloc("atsb%d" % i, 4 * 2 * 512, BF16) for i in range(2)]
            fbd = [alloc("fbd%d" % i, 8 * 384, BF16) for i in range(2)]
            f2_3 = f2.ap.rearrange("p (a n) -> p a n", a=64)
            bsb3 = bsb.ap.rearrange("p (c n) -> p c n", c=128)
            b_bsb = [P.buf("bsbq%d" % i) for i in range(32)]
            p_b = [pbank(i, "p_b%d" % i) for i in range(2)]
            p_a = [pbank(2 + i, "p_a%d" % i) for i in range(4)]
            p_r = [pbank(6 + i, "p_r%d" % i) for i in range(2)]
            for fb in fbd:
                P.op("pool", lambda e, fb=fb: e.memset(fb.ap, 0.0), writes=[fb.b])
            u_v = u_d.rearrange("(a b) c -> a b c", b=32)
            uall4 = uall.ap.rearrange("p (c g b) -> p g c b", c=128, g=4)
            b_u2 = []
            for i in range(4):
                stg = atsb[i % 2]
                P.op("sp", lambda e, i=i, stg=stg: e.dma_start(
                    out=stg.ap.rearrange("p (b c) -> p b c", b=8), in_=u_v[:, 8 * i:8 * i + 8, :]),
                    writes=[stg.b], dma=stg.b)
                stg4 = stg.ap.rearrange("p (b g c) -> p g c b", b=8, g=4)
                for g in range(4):
                    bq = P.buf("u2_%d_%d" % (i, g))
                    b_u2.append(bq)
                    eng = ("dve", "act", "dve", "act")[g]
                    if eng == "act":
                        P.op("act", lambda e, i=i, g=g, stg4=stg4: e.activation(
                            out=uall4[:, g, :, 8 * i:8 * i + 8], in_=stg4[:, g], func=ACT.Copy),
                            reads=[stg.b], writes=[bq])
                    else:
                        P.op(eng, lambda e, i=i, g=g, stg4=stg4: e.tensor_copy(
                            out=uall4[:, g, :, 8 * i:8 * i + 8], in_=stg4[:, g]),
                            reads=[stg.b], writes=[bq])
            for half in range(2):
                P.op("sp", lambda e, half=half: e.dma_start(
                    out=f2.ap, in_=f2_d[:, 6144 * half:6144 * half + 6144]), writes=[f2.b], dma=f2.b)
                for k in range(32):
                    pb_ = p_b[k % 2]

                    def s1f(e, k=k, pb_=pb_, half=half):
                        i = None
                        for j in range(4):
                            cq = 4 * k + j
                            i = e.matmul(pb_.ap[:, 128 * j:128 * j + 128], uall.ap[:, 128 * cq:128 * cq + 128],
                                         f1.ap[:, 128 * half:128 * half + 128], start=True, stop=True)
                        return i
                    P.op("pe", s1f, reads=b_u2 + [f1.b], writes=[pb_.b])
                    dst = bsb.ap[:, 512 * k:512 * k + 512]
                    if k % 2 == 0:
                        P.op("act", lambda e, dst=dst, pb_=pb_: e.activation(out=dst, in_=pb_.ap, func=ACT.Copy),
                             reads=[pb_.b], writes=[b_bsb[k]])
                    else:
                        P.op("dve", lambda e, dst=dst, pb_=pb_: e.tensor_copy(out=dst, in_=pb_.ap),
                             reads=[pb_.b], writes=[b_bsb[k]])
                for cb in range(4):
                    at_ = atsb[cb % 2]
                    at5 = at_.ap.rearrange("p (j g r s) -> p g r j s", j=16, g=4, r=2)
                    b_at_parts = []
                    for sb in range(2):
                        bb = 2 * cb + sb
                        fb = fbd[bb % 2]
                        fb4 = fb.ap.rearrange("p (l g n) -> p l g n", l=8, g=4)
                        b_fb = [P.buf("fbq%d_%d" % (bb % 2, g)) for g in range(4)]
                        for g in range(4):
                            src = f2_3[32 * g:32 * g + 32, 8 * bb:8 * bb + 8, :]
                            dst = fb4[32 * g:32 * g + 32, :, g, :]
                            eng = ("pool", "act", "act", "act")[g]
                            if eng == "act":
                                P.op("act", lambda e, src=src, dst=dst: e.activation(out=dst, in_=src, func=ACT.Copy),
                                     reads=[f2.b, fb.b], writes=[b_fb[g]])
                            else:
                                P.op(eng, lambda e, src=src, dst=dst: e.tensor_copy(out=dst, in_=src),
                                     reads=[f2.b, fb.b], writes=[b_fb[g]])
                        for pr in range(4):
                            def s2f(e, bb=bb, pr=pr, fb4=fb4):
                                i = None
                                for l2 in range(2):
                                    l = 2 * pr + l2
                                    s1l = 8 * bb + l
                                    for ri in range(2):
                                        off = 32 if ri == 0 else 0
                                        i = e.matmul(p_a[pr].ap[:, 256 * l2:256 * l2 + 256],
                                                     bsb3[:, :, 64 * ri + s1l],
                                                     fb4[:, l, :, off:off + 64],
                                                     start=(ri == 0), stop=(ri == 1))
                                return i
                            P.op("pe", s2f, reads=b_bsb + b_fb, writes=[p_a[pr].b, fb.b])
                            src = p_a[pr].ap
                            j0 = 8 * sb + 2 * pr
                            dst = at_.ap[:, 256 * j0:256 * j0 + 512]
                            b_atp = P.buf("atp")
                            b_at_parts.append(b_atp)
                            if pr % 2 == 0:
                                P.op("act", lambda e, src=src, dst=dst: e.activation(out=dst, in_=src, func=ACT.Copy),
                                     reads=[p_a[pr].b, at_.b], writes=[b_atp])
                            else:
                                P.op("dve", lambda e, src=src, dst=dst: e.tensor_copy(out=dst, in_=src),
                                     reads=[p_a[pr].b, at_.b], writes=[b_atp])
                    s1p0 = 64 * half + 16 * cb
                    for g in range(4):
                        pr_ = p_r[g % 2]

                        def cm(e, g=g, pr_=pr_, at5=at5):
                            e.matmul(pr_.ap, wcs4[:, g, 0, :], at5[:, g, 0, :, :], start=True, stop=False)
                            return e.matmul(pr_.ap, wcs4[:, g, 1, :], at5[:, g, 1, :, :], start=False, stop=True)
                        P.op("pe", cm, reads=b_at_parts + [wcs.b], writes=[pr_.b] + ([at_.b] if g == 3 else []))
                        view = bass.AP(sgT.ap.tensor, sgT3[:, g, s1p0:s1p0 + 1].offset,
                                       [list(sgT.ap.ap[0]), [128, 32], [1, 16]])
                        b_sg = P.buf("sg_%d" % g)
                        P.op("dve", lambda e, view=view, pr_=pr_: e.tensor_tensor(
                            out=view, in0=pr_.ap.rearrange("p (j s) -> p s j", j=16), in1=view, op=ALU.mult),
                            reads=[pr_.b], writes=[b_sg])
            P.barrier()
            if debug and slot == 0 and layer == 0:
                dma(dbg["mixf"][:, :], sgT.ap, sgT, False)
                P.barrier()

            if stop == "C":
                break
            astate["off"] = A_MARK
            xd = [alloc("xd%d" % i, 1024, F32) for i in range(3)]
            p_yd = [[pbank(2 * i + hf, "p_yd%d_%d" % (i, hf)) for hf in range(2)] for i in range(4)]

            def D_load(t):
                dma(xd[t % 3].ap, xsrc[t * 128:(t + 1) * 128, :], xd[t % 3], True)
            D_load(0)
            D_load(1)
            for t in range(NT):
                y_ = xd[t % 3]
                if t + 2 < NT:
                    D_load(t + 2)
                for hf in range(2):
                    py = p_yd[t % 4][hf]

                    def od(e, t=t, hf=hf, py=py):
                        i = None
                        for j in range(8):
                            src = mTa3 if j < 4 else sgT3
                            i = e.matmul(py.ap, src[:, j % 4, 128 * t:128 * t + 128],
                                         w_out3[:, j, 512 * hf:512 * hf + 512],
                                         start=(j == 0), stop=(j == 7))
                        return i
                    P.op("pe", od, reads=[sgT.b, mTa.b, w_out_bf.b], writes=[py.b])
                    P.op("dve", lambda e, hf=hf, y_=y_, py=py: e.tensor_tensor(
                        out=y_.ap[:, 512 * hf:512 * hf + 512], in0=py.ap,
                        in1=y_.ap[:, 512 * hf:512 * hf + 512], op=ALU.add),
                        reads=[py.b, y_.b], writes=[y_.b])
                dma(ydst[128 * t:128 * t + 128, :], y_.ap, y_, False)
            P.barrier()
    P.emit()
    return nc, P


_CACHE = {}


def _host_inputs(x_slots, norm_gain, w_in, q_norm_gain, k_norm_gain, sink_logit, w_fourier, w_out):
    c = _consts()
    gain_d = np.ascontiguousarray(norm_gain.reshape(2, 8, 128).transpose(0, 2, 1)).astype(np.float32)
    qkg = np.concatenate([np.tile(q_norm_gain, (1, 8)), np.tile(k_norm_gain, (1, 2))], axis=1)
    qkg = np.ascontiguousarray(qkg.reshape(2, 1, 640)).astype(np.float32)
    sink = np.ascontiguousarray(sink_logit.reshape(2, 1, 8)).astype(np.float32)
    base = {
        "w_in": np.ascontiguousarray(w_in, dtype=np.float32),
        "w_out": np.ascontiguousarray(w_out, dtype=np.float32),
        "gain_d": gain_d, "qkg": qkg, "sink": sink,
        "w_four": np.ascontiguousarray(w_fourier, dtype=np.float32),
        "ident": c["ident"], "f1": c["f1"], "f2": c["f2"], "cs128": c["cs128"],
        "cosr": c["cosr"], "sinr": c["sinr"], "masks": c["masks"],
    }
    maps = []
    for xs in x_slots:
        m = dict(base)
        m["x"] = xs
        maps.append(m)
    return maps


def kernel(x_prompt, x_sample, norm_gain, w_in, q_norm_gain, k_norm_gain, sink_logit, w_fourier, w_out):
    x_prompt = np.asarray(x_prompt, dtype=np.float32)
    x_sample = np.asarray(x_sample, dtype=np.float32)
    if "nc" not in _CACHE:
        _CACHE["nc"] = build(2, 2)[0]
    nc = _CACHE["nc"]
    x_slots = []
    for c in range(N_CORES):
        s0 = x_sample[c]
        s1 = x_prompt[c] if c < 4 else x_sample[c]
        x_slots.append(np.ascontiguousarray(np.stack([s0, s1], axis=0)))
    maps = _host_inputs(x_slots, np.asarray(norm_gain), np.asarray(w_in), np.asarray(q_norm_gain),
                        np.asarray(k_norm_gain), np.asarray(sink_logit), np.asarray(w_fourier),
                        np.asarray(w_out))
    res = run_bass_kernel_spmd(nc, maps, core_ids=list(range(N_CORES)))
    ys = [np.asarray(r["y"]) for r in res.results]
    y_sample = np.stack([ys[c][0] for c in range(N_CORES)], axis=0).astype(np.float32)
    y_prompt = np.stack([ys[c][1] for c in range(4)], axis=0).astype(np.float32)
    return (y_prompt, y_sample)
```
